# Optimizing a Trainium2 kernel written in Bass

```python
import jax, jax.numpy as jnp
from jax import lax
import numpy as np

D_MODEL = 1024
BATCH = 4
SEQ = 4096
DEPTH = 2

CTX_LEN = 256
GRID_W = 64
RMS_EPS = 1e-6
D_FF = 4 * D_MODEL

RW_HEADS = 8
RW_HEAD_DIM = 64
RW_WIDTH = RW_HEADS * RW_HEAD_DIM
RW_DECAY_RANK = 64
RW_ICLR_RANK = 64
RW_GATE_RANK = 128
RW_VRES_RANK = 32
RW_GN_EPS = 64e-5
RW_SPLIT = (RW_WIDTH, 2 * RW_WIDTH, 3 * RW_WIDTH,
            3 * RW_WIDTH + RW_DECAY_RANK,
            3 * RW_WIDTH + 2 * RW_DECAY_RANK,
            3 * RW_WIDTH + 2 * RW_DECAY_RANK + RW_ICLR_RANK,
            3 * RW_WIDTH + 2 * RW_DECAY_RANK + 2 * RW_ICLR_RANK)
RW_COLS = 3 * RW_WIDTH + 2 * RW_DECAY_RANK + 2 * RW_ICLR_RANK + RW_GATE_RANK

GLA_HEADS = 4
GLA_DK = 64
GLA_DV = 128
GLA_KW = GLA_HEADS * GLA_DK
GLA_VW = GLA_HEADS * GLA_DV
GLA_QKV_W = 2 * GLA_KW + GLA_VW
GLA_GATE_RANK = 16
GLA_TAU = 16.0
GLA_CHUNK = 64
GLA_SPLIT = (GLA_QKV_W, GLA_QKV_W + GLA_GATE_RANK, GLA_QKV_W + 2 * GLA_GATE_RANK)
GLA_COLS = GLA_QKV_W + 2 * GLA_GATE_RANK + GLA_VW

IN_SPLIT = (RW_COLS, RW_COLS + GLA_COLS)
IN_COLS = RW_COLS + GLA_COLS + 2 * D_MODEL

kernel_name = "hybrid_rwkv7_gla_dit_block"


def rms_norm(x, gain):
    xf = x.astype(jnp.float32)
    y = xf * lax.rsqrt(jnp.mean(xf * xf, axis=-1, keepdims=True) + RMS_EPS)
    return (y * gain).astype(x.dtype)


def modulate(x, gain, shift, scale):
    return rms_norm(x, gain) * (1.0 + scale) + shift


def shift_prev(t):
    return jnp.pad(t[:, :-1], ((0, 0), (1, 0), (0, 0)))


def shift_next(t):
    return jnp.pad(t[:, 1:], ((0, 0), (0, 1), (0, 0)))


def centred_conv3(t, w):
    return w[0] * shift_prev(t) + w[1] * t + w[2] * shift_next(t)


def to_scan_order(t, order, rows):
    if order == "row":
        return t
    b_, n_, f_ = t.shape
    return t.reshape(b_, rows, GRID_W, f_).transpose(0, 2, 1, 3).reshape(b_, n_, f_)


def from_scan_order(t, order, rows):
    if order == "row":
        return t
    b_, n_, f_ = t.shape
    return t.reshape(b_, GRID_W, rows, f_).transpose(0, 2, 1, 3).reshape(b_, n_, f_)


def rwkv7_scan(r, decay, k, v, kk, a, s0, reverse):
    def step(s, inp):
        r_t, w_t, k_t, v_t, kk_t, a_t = inp
        sa = jnp.einsum("bhvk,bhk->bhv", s, kk_t)
        s = (s * w_t[:, :, None, :]
             - sa[..., None] * (kk_t * a_t)[:, :, None, :]
             + v_t[..., None] * k_t[:, :, None, :])
        return s, jnp.einsum("bhvk,bhk->bhv", s, r_t)
    xs = tuple(jnp.moveaxis(t.astype(jnp.float32), 1, 0) for t in (r, decay, k, v, kk, a))
    s_final, ys = lax.scan(step, s0, xs, reverse=reverse)
    return jnp.moveaxis(ys, 0, 1), s_final


def rwkv7_branch(f, v_first, s0, lp):
    b_, t_, _ = f.shape
    heads = lambda t: t.reshape(b_, t_, RW_HEADS, RW_HEAD_DIM)
    f = f + lp["rw_mu"] * (0.5 * (shift_prev(f) + shift_next(f)) - f)
    r, k, v, wd_f, wd_b, ad_f, ad_b, gd = jnp.split(f, RW_SPLIT, axis=-1)
    if v_first is not None:
        v = v + (v_first - v) * jax.nn.sigmoid(
            lp["rw_vres_bias"] + (v @ lp["rw_vres_down"]) @ lp["rw_vres_up"])
    kk = heads(k * lp["rw_k_k"]).astype(jnp.float32)
    kk = kk * lax.rsqrt(jnp.sum(kk * kk, axis=-1, keepdims=True) + 1e-12)

    def direction(d, wd, ad, reverse):
        w = -jax.nn.softplus(-(lp["rw_w0"][d] + jnp.tanh(wd) @ lp["rw_w_up"][d])) - 0.5
        decay = jnp.exp(-jnp.exp(w.astype(jnp.float32)))
        a = jax.nn.sigmoid(lp["rw_a0"][d] + ad @ lp["rw_a_up"][d])
        k_rep = k * (1.0 + (a - 1.0) * lp["rw_k_a"])
        y, s = rwkv7_scan(heads(r), heads(decay), heads(k_rep), heads(v), kk, heads(a), s0[d], reverse)
        return y, s, k_rep

    y_f, s_f, k_f = direction(0, wd_f, ad_f, False)
    y_b, s_b, k_b = direction(1, wd_b, ad_b, True)
    y = y_f + y_b
    mean = jnp.mean(y, axis=-1, keepdims=True)
    var = jnp.mean(jnp.square(y - mean), axis=-1, keepdims=True)
    y = ((y - mean) * lax.rsqrt(var + RW_GN_EPS) * lp["rw_gn_w"].reshape(RW_HEADS, RW_HEAD_DIM)
         + lp["rw_gn_b"].reshape(RW_HEADS, RW_HEAD_DIM))
    bonus = jnp.sum(heads(r) * heads(k_f + k_b) * lp["rw_r_k"], axis=-1, keepdims=True) * heads(v)
    g = jax.nn.sigmoid(gd) @ lp["rw_g_up"]
    out = (y + bonus).reshape(b_, t_, RW_WIDTH) * g
    return out.astype(f.dtype), (s_f, s_b), v


def gla_chunked(q, k, v, log_alpha, s0):
    b_, t_, h_, _ = q.shape
    nc = t_ // GLA_CHUNK
    chunk = lambda t: t.reshape(b_, nc, GLA_CHUNK, h_, t.shape[-1])
    qc, kc, vc, gc = chunk(q), chunk(k), chunk(v), chunk(log_alpha)
    cum = jnp.cumsum(gc, axis=2)
    cum_last = cum[:, :, -1:]
    q_dec = qc * jnp.exp(cum)
    k_inv = kc * jnp.exp(-cum)
    k_to_end = kc * jnp.exp(cum_last - cum)
    causal = jnp.tril(jnp.ones((GLA_CHUNK, GLA_CHUNK), dtype=bool))
    scores = jnp.where(causal, jnp.einsum("bnthd,bnshd->bnhts", q_dec, k_inv), 0.0)
    o_intra = jnp.einsum("bnhts,bnshv->bnthv", scores, vc)
    chunk_kv = jnp.einsum("bnshd,bnshv->bnhdv", k_to_end, vc)
    chunk_decay = jnp.exp(cum_last[:, :, 0])

    def step(s, inp):
        dec, kv = inp
        return s * dec[..., None] + kv, s
    s_final, s_start = lax.scan(step, s0, (jnp.moveaxis(chunk_decay, 1, 0), jnp.moveaxis(chunk_kv, 1, 0)))
    o_inter = jnp.einsum("bnthd,bnhdv->bnthv", q_dec, jnp.moveaxis(s_start, 0, 1))
    return (o_intra + o_inter).reshape(b_, t_, h_, v.shape[-1]), s_final


def gla_branch(f, s0, lp):
    b_, t_, _ = f.shape
    qkv, ald_f, ald_b, gate = jnp.split(f, GLA_SPLIT, axis=-1)
    qkv = jax.nn.silu(centred_conv3(qkv, lp["gla_conv"]))
    q, k, v = jnp.split(qkv, (GLA_KW, 2 * GLA_KW), axis=-1)
    heads_k = lambda t: t.reshape(b_, t_, GLA_HEADS, GLA_DK).astype(jnp.float32)
    q = heads_k(q) * (GLA_DK ** -0.5)
    k = heads_k(k)
    v = v.reshape(b_, t_, GLA_HEADS, GLA_DV).astype(jnp.float32)
    log_alpha = lambda d, ad: heads_k(
        jax.nn.log_sigmoid(ad @ lp["gla_alpha_up"][d] + lp["gla_alpha_bias"][d])) / GLA_TAU
    flip = lambda t: t[:, ::-1]
    o_f, s_f = gla_chunked(q, k, v, log_alpha(0, ald_f), s0[0])
    o_b, s_b = gla_chunked(flip(q), flip(k), flip(v), flip(log_alpha(1, ald_b)), s0[1])
    o = o_f + flip(o_b)
    o = o * lax.rsqrt(jnp.mean(o * o, axis=-1, keepdims=True) + RMS_EPS) * lp["gla_norm_w"]
    out = o.reshape(b_, t_, GLA_VW) * jax.nn.silu(gate)
    return out.astype(f.dtype), (s_f, s_b)


def merge_branches(y_rw, y_gla, gate_cols, lp):
    g_rw, g_gla = jnp.split(jax.nn.sigmoid(gate_cols), 2, axis=-1)
    return (g_rw * (y_rw @ lp["rw_out"]) + g_gla * (y_gla @ lp["gla_out"])) @ lp["merge_out"]


def token_mixer(h_lat, h_ctx, v_first, lp, layer, rows, need_ctx_out):
    rw_order, gla_order = ("row", "col") if layer % 2 == 0 else ("col", "row")
    rw_lat, gla_lat, gate_lat = jnp.split(h_lat @ lp["w_in"], IN_SPLIT, axis=-1)
    rw_ctx, gla_ctx, gate_ctx = jnp.split(h_ctx @ lp["w_in"], IN_SPLIT, axis=-1)
    vf_lat, vf_ctx = v_first
    b_ = h_lat.shape[0]
    z_rw = jnp.zeros((b_, RW_HEADS, RW_HEAD_DIM, RW_HEAD_DIM), jnp.float32)
    z_gla = jnp.zeros((b_, GLA_HEADS, GLA_DK, GLA_DV), jnp.float32)
    y_rw_ctx, s_rw, v_ctx = rwkv7_branch(rw_ctx, vf_ctx, (z_rw, z_rw), lp)
    y_gla_ctx, s_gla = gla_branch(gla_ctx, (z_gla, z_gla), lp)
    vf_lat_o = None if vf_lat is None else to_scan_order(vf_lat, rw_order, rows)
    y_rw_lat, _, v_lat = rwkv7_branch(to_scan_order(rw_lat, rw_order, rows), vf_lat_o, s_rw, lp)
    y_rw_lat = from_scan_order(y_rw_lat, rw_order, rows)
    v_lat = from_scan_order(v_lat, rw_order, rows)
    y_gla_lat, _ = gla_branch(to_scan_order(gla_lat, gla_order, rows), s_gla, lp)
    y_gla_lat = from_scan_order(y_gla_lat, gla_order, rows)
    m_lat = merge_branches(y_rw_lat, y_gla_lat, gate_lat, lp)
    m_ctx = merge_branches(y_rw_ctx, y_gla_ctx, gate_ctx, lp) if need_ctx_out else None
    return m_lat, m_ctx, (v_lat, v_ctx)


def sq_relu_mlp(h, w1, w2):
    return jnp.square(jax.nn.relu(h @ w1)) @ w2


def setup_inputs(seed: int = 0) -> dict:
    key = jax.random.key(seed)
    ks = list(jax.random.split(key, 40))
    nrm = lambda shape, s: jax.random.normal(ks.pop(), shape, jnp.float32) * s
    uni = lambda shape, lo, hi: jax.random.uniform(ks.pop(), shape, jnp.float32, lo, hi)
    L, D = DEPTH, D_MODEL
    return {
        "x": nrm((BATCH, SEQ, D), 1.0),
        "c": nrm((BATCH, D), 1.0),
        "ctx": nrm((BATCH, CTX_LEN, D), 1.0),
        "c_ctx": nrm((D,), 1.0),
        "w_in": nrm((L, D, IN_COLS), D ** -0.5),
        "rw_mu": uni((L, RW_COLS), 0.0, 1.0),
        "rw_w0": uni((L, 2, RW_WIDTH), -6.0, 1.0),
        "rw_w_up": nrm((L, 2, RW_DECAY_RANK, RW_WIDTH), 0.1),
        "rw_a0": nrm((L, 2, RW_WIDTH), 0.3),
        "rw_a_up": nrm((L, 2, RW_ICLR_RANK, RW_WIDTH), 0.1),
        "rw_g_up": nrm((L, RW_GATE_RANK, RW_WIDTH), RW_GATE_RANK ** -0.5),
        "rw_k_k": 0.85 + nrm((L, RW_WIDTH), 0.05),
        "rw_k_a": 1.0 + nrm((L, RW_WIDTH), 0.05),
        "rw_r_k": nrm((L, RW_HEADS, RW_HEAD_DIM), 0.1),
        "rw_gn_w": 1.0 + nrm((L, RW_WIDTH), 0.02),
        "rw_gn_b": nrm((L, RW_WIDTH), 0.02),
        "rw_vres_down": nrm((L - 1, RW_WIDTH, RW_VRES_RANK), RW_WIDTH ** -0.5),
        "rw_vres_up": nrm((L - 1, RW_VRES_RANK, RW_WIDTH), RW_VRES_RANK ** -0.5),
        "rw_vres_bias": nrm((L - 1, RW_WIDTH), 0.1),
        "rw_out": nrm((L, RW_WIDTH, D), RW_WIDTH ** -0.5),
        "gla_conv": nrm((L, 3, GLA_QKV_W), 0.6),
        "gla_alpha_up": nrm((L, 2, GLA_GATE_RANK, GLA_KW), GLA_GATE_RANK ** -0.5),
        "gla_alpha_bias": uni((L, 2, GLA_KW), 0.0, 4.0),
        "gla_norm_w": 1.0 + nrm((L, GLA_DV), 0.02),
        "gla_out": nrm((L, GLA_VW, D), GLA_VW ** -0.5),
        "merge_out": nrm((L, D, D), D ** -0.5),
        "mlp_w1": nrm((L, D, D_FF), D ** -0.5),
        "mlp_w2": nrm((L, D_FF, D), D_FF ** -0.5),
        "ada_w": nrm((L, D, 6 * D), 0.5 * D ** -0.5),
        "ada_b": nrm((L, 6 * D), 0.02),
        "norm_mix_pre": 1.0 + nrm((L, D), 0.02),
        "norm_mix_post": 1.0 + nrm((L, D), 0.02),
        "norm_ffn_pre": 1.0 + nrm((L, D), 0.02),
        "norm_ffn_post": 1.0 + nrm((L, D), 0.02),
    }


def reference(x, c, ctx, c_ctx, w_in, rw_mu, rw_w0, rw_w_up, rw_a0, rw_a_up, rw_g_up,
              rw_k_k, rw_k_a, rw_r_k, rw_gn_w, rw_gn_b, rw_vres_down, rw_vres_up, rw_vres_bias,
              rw_out, gla_conv, gla_alpha_up, gla_alpha_bias, gla_norm_w, gla_out, merge_out,
              mlp_w1, mlp_w2, ada_w, ada_b, norm_mix_pre, norm_mix_post, norm_ffn_pre,
              norm_ffn_post):
    rows = x.shape[1] // GRID_W
    x_lat, x_ctx = x, ctx
    v_first = (None, None)
    for l in range(DEPTH):
        last = l == DEPTH - 1
        lp = {"w_in": w_in[l], "rw_mu": rw_mu[l], "rw_w0": rw_w0[l], "rw_w_up": rw_w_up[l],
              "rw_a0": rw_a0[l], "rw_a_up": rw_a_up[l], "rw_g_up": rw_g_up[l],
              "rw_k_k": rw_k_k[l], "rw_k_a": rw_k_a[l], "rw_r_k": rw_r_k[l],
              "rw_gn_w": rw_gn_w[l], "rw_gn_b": rw_gn_b[l], "rw_out": rw_out[l],
              "gla_conv": gla_conv[l], "gla_alpha_up": gla_alpha_up[l],
              "gla_alpha_bias": gla_alpha_bias[l], "gla_norm_w": gla_norm_w[l],
              "gla_out": gla_out[l], "merge_out": merge_out[l]}
        if l > 0:
            lp["rw_vres_down"] = rw_vres_down[l - 1]
            lp["rw_vres_up"] = rw_vres_up[l - 1]
            lp["rw_vres_bias"] = rw_vres_bias[l - 1]
        mod_lat = (jax.nn.silu(c) @ ada_w[l] + ada_b[l])[:, None, :]
        mod_ctx = jax.nn.silu(c_ctx) @ ada_w[l] + ada_b[l]
        sh_m, sc_m, g_m, sh_f, sc_f, g_f = jnp.split(mod_lat, 6, axis=-1)
        csh_m, csc_m, cg_m, csh_f, csc_f, cg_f = jnp.split(mod_ctx, 6, axis=-1)

        h_lat = modulate(x_lat, norm_mix_pre[l], sh_m, sc_m)
        h_ctx = modulate(x_ctx, norm_mix_pre[l], csh_m, csc_m)
        m_lat, m_ctx, v_new = token_mixer(h_lat, h_ctx, v_first, lp, l, rows, not last)
        if l == 0:
            v_first = v_new
        x_lat = x_lat + g_m * rms_norm(m_lat, norm_mix_post[l])
        f_lat = sq_relu_mlp(modulate(x_lat, norm_ffn_pre[l], sh_f, sc_f), mlp_w1[l], mlp_w2[l])
        x_lat = x_lat + g_f * rms_norm(f_lat, norm_ffn_post[l])
        if not last:
            x_ctx = x_ctx + cg_m * rms_norm(m_ctx, norm_mix_post[l])
            f_ctx = sq_relu_mlp(modulate(x_ctx, norm_ffn_pre[l], csh_f, csc_f), mlp_w1[l], mlp_w2[l])
            x_ctx = x_ctx + cg_f * rms_norm(f_ctx, norm_ffn_post[l])
    return x_lat
```

```python
import numpy as np
from contextlib import ExitStack
import concourse.bass as bass
import concourse.mybir as mybir
from concourse.bass_utils import run_bass_kernel_spmd

F32 = mybir.dt.float32
F32R = mybir.dt.float32r
BF16 = mybir.dt.bfloat16
AF = mybir.ActivationFunctionType
ALU = mybir.AluOpType
AX = mybir.AxisListType

D = 1024
TC = 256
TL = 4096
OFFC = 1
OFFL = 259
TW = 4356
EPS = 1e-6
C0_RW = float(np.exp(-0.5))
C0_GLA = 1.0 / 16.0


class Buf:
    def __init__(self, t, name):
        self.t = t
        self.name = name
        self.w = {}
        self.r = {}
        self.sem = None
        self.semval = 0
        self.sidx = None
        self.sphase = -1

    def __getitem__(self, k):
        return self.t[k]


class View(Buf):
    def __init__(self, base, ap):
        self.base = base
        self.t = ap
        self.name = base.name

    w = property(lambda s: s.base.w, lambda s, v: setattr(s.base, "w", v))
    r = property(lambda s: s.base.r, lambda s, v: setattr(s.base, "r", v))
    sidx = property(lambda s: s.base.sidx, lambda s, v: setattr(s.base, "sidx", v))
    sphase = property(lambda s: s.base.sphase, lambda s, v: setattr(s.base, "sphase", v))


class Sched:
    def __init__(self, nc, es):
        self.nc = nc
        self.es = es
        self.engs = {"pe": nc.tensor, "act": nc.scalar, "dve": nc.vector, "pool": nc.gpsimd, "sp": nc.sync}
        self.sem = {}
        self.cnt = {}
        for e in ("pe", "act", "dve", "pool"):
            self.sem[e] = es.enter_context(nc.semaphore("s_" + e))
            self.cnt[e] = 0
        self.waited = {}
        self.latest = {}
        self.nsem = 0
        self.dpool = []
        self.dfree = {"hw": [], "sw": []}
        self.phase = 0

    def _wait(self, e, toks):
        eng = self.engs[e]
        for name, (sem, val) in toks.items():
            key = (e, name)
            if self.waited.get(key, 0) < val:
                eng.wait_ge(sem, val)
                self.waited[key] = val

    def _deps(self, e, R, W):
        toks = {}

        def add(d):
            for k, (s, v) in d.items():
                if k not in toks or toks[k][1] < v:
                    toks[k] = (s, v)
        for b in R:
            add(b.w)
        for b in W:
            add(b.w)
            add(b.r)
        self._wait(e, toks)

    def op(self, e, R, W, fn):
        self._deps(e, R, W)
        ins = fn(self.engs[e])
        self.cnt[e] += 1
        ins.then_inc(self.sem[e], 1)
        name = "s_" + e
        tok = (self.sem[e], self.cnt[e])
        self.latest[name] = tok
        for b in R:
            b.r[name] = tok
        for b in W:
            b.w = {name: tok}
            b.r = {}
        return ins

    def dma(self, q, out, in_, R, W, owner, **kw):
        self._deps(q, R, W)
        if owner.sidx is None or owner.sphase != self.phase:
            kind = "sw" if q == "pool" else "hw"
            if self.dfree[kind]:
                owner.sidx = self.dfree[kind].pop()
            else:
                self.nsem += 1
                nm = "dq%d" % self.nsem
                self.dpool.append([self.es.enter_context(self.nc.semaphore(nm)), 0, nm, kind])
                owner.sidx = len(self.dpool) - 1
            owner.sphase = self.phase
        ent = self.dpool[owner.sidx]
        ent[1] += 16
        ins = self.engs[q].dma_start(out=out, in_=in_, **kw)
        ins.then_inc(ent[0], 16)
        name = ent[2]
        tok = (ent[0], ent[1])
        self.latest[name] = tok
        for b in R:
            b.r[name] = tok
        for b in W:
            b.w = {name: tok}
            b.r = {}
        return ins

    def barrier(self):
        for e in self.engs:
            self._wait(e, dict(self.latest))
        self.phase += 1
        self.dfree = {k: [i for i, ent in enumerate(self.dpool) if ent[3] == k] for k in ("hw", "sw")}


class Ctx:
    pass


class StopPhase(Exception):
    pass


def build(dbg=None):
    nc = bass.Bass("TRN2", target_bir_lowering=False)
    es = ExitStack()
    S = Sched(nc, es)
    skind = "ExternalOutput" if dbg else "Internal"
    stop = dbg.get("stop") if dbg else None
    nlayers = dbg.get("nlayers", 2) if dbg else 2
    ntl = dbg.get("ntiles", 17) if dbg else 17
    ntlA = dbg.get("ntilesA", 17) if dbg else 17
    stage_lim = dbg.get("stage", 99) if dbg else 99
    ntlC = dbg.get("ntilesC", 99) if dbg else 99

    def stage(n):
        if n >= stage_lim:
            raise StopPhase()

    def din(name, shape, dt=F32):
        return nc.dram_tensor(name, list(shape), dt, kind="ExternalInput").ap()

    def dscr(name, shape, dt=F32):
        return Buf(nc.dram_tensor(name, list(shape), dt, kind=skind).ap(), name)

    I = Ctx()
    I.x_lat = Buf(din("x_lat", [TL, D]), "x_lat")
    I.x_ctx = Buf(din("x_ctx", [TC, D]), "x_ctx")
    I.cfm = din("cfm", [128, 8, 2])
    I.ada_w = din("ada_w", [2, D, 6 * D])
    I.adab_fm = din("adab_fm", [2, 128, 48])
    I.ada_b = din("ada_b", [2, 6 * D])
    I.w_in = din("w_in", [2, D, 5536])
    I.npre_fm = din("npre_fm", [2, 128, 8, 2])
    I.npost = din("npost", [2, 2, D])
    I.rw_mu_fm = din("rw_mu_fm", [2, 128, 15])
    I.rw_w0_fm = din("rw_w0_fm", [2, 2, 128, 4])
    I.rw_a0_fm = din("rw_a0_fm", [2, 2, 128, 4])
    I.rw_w_up = din("rw_w_up", [2, 128, 512])
    I.rw_a_up = din("rw_a_up", [2, 128, 512])
    I.rw_g_up = din("rw_g_up", [2, 128, 512])
    I.kk_fm = din("kk_fm", [2, 128, 4, 3])
    I.rw_gn = din("rw_gn", [2, 2, 512])
    I.vres_down = din("vres_down", [128, 4, 32])
    I.vres_up = din("vres_up", [32, 512])
    I.vres_bias = din("vres_bias", [1, 512])
    I.rw_out = din("rw_out", [2, 512, D])
    I.gla_out = din("gla_out", [2, 512, D])
    I.merge_out = din("merge_out", [2, D, D])
    I.mlp_w1 = din("mlp_w1", [2, D, 4 * D])
    I.mlp_w2 = din("mlp_w2", [2, 4 * D, D])
    I.gla_conv_fm = din("gla_conv_fm", [2, 128, 8, 3])
    I.gla_alpha_up = din("gla_alpha_up", [2, 2, 16, 256])
    I.gla_abias_fm = din("gla_abias_fm", [2, 2, 128, 2])
    I.gla_norm_w = din("gla_norm_w", [2, 128])
    out_t = Buf(nc.dram_tensor("out", [TL, D], F32, kind="ExternalOutput").ap(), "out")

    X = Ctx()
    X.rawrw = dscr("rawrw", [15 * 128, TW])
    X.rawgl = dscr("rawgl", [10 * 128, TW])
    X.gatetm = dscr("gatetm", [TC + TL, 512])
    X.yrwf = dscr("yrwf", [2, TC + TL, 256])
    X.yrwb = dscr("yrwb", [2, TC + TL, 4])
    X.yrw = dscr("yrw", [TC + TL, 512])
    X.yglf = dscr("yglf", [2, TC + TL, 256])
    X.ygl = dscr("ygl", [TC + TL, 512])
    X.vfirst = dscr("vfirst", [TC + TL, 512])
    X.xm_lat = dscr("xm_lat", [TL, D])
    X.xm_ctx = dscr("xm_ctx", [TC, D])
    X.x1_lat = dscr("x1_lat", [TL, D])
    X.x1_ctx = dscr("x1_ctx", [TC, D])
    X.Gd = dscr("Gd", [128, 4 * D])

    uniq = [0]

    def sb(stack, name, shape, dt=F32):
        uniq[0] += 1
        name = "%s_%d" % (name, uniq[0])
        return Buf(stack.enter_context(nc.sbuf_tensor(name, list(shape), dt)), name)

    def psb(stack, name, shape, dt=F32):
        uniq[0] += 1
        name = "%s_%d" % (name, uniq[0])
        return Buf(stack.enter_context(nc.psum_tensor(name, list(shape), dt)), name)

    P = Ctx()
    P.ident_f = sb(es, "ident_f", [128, 128])
    P.ident_b = sb(es, "ident_b", [128, 128], BF16)
    P.ones_f = sb(es, "ones_f", [128, 128])
    P.zeros_f = sb(es, "zeros_f", [128, 512])
    P.bones = sb(es, "bones", [128, 128])
    P.m_ge_f = sb(es, "m_ge_f", [128, 4, 64])
    P.m_gt_f = sb(es, "m_gt_f", [128, 4, 64])
    P.m_ge_b = sb(es, "m_ge_b", [128, 4, 64])
    P.m_gt_b = sb(es, "m_gt_b", [128, 4, 64])
    P.rmask = sb(es, "rmask", [128, 256])
    P.cfm = sb(es, "cfm_s", [128, 8, 2])
    P.scs = sb(es, "scs", [128, 8, 2])
    P.modF = sb(es, "modF", [128, 48, 2])
    P.SB_ = sb(es, "SBm", [128, 8, 2, 2, 2])
    P.eps = sb(es, "epsc", [128, 4])

    def consts():
        S.op("pool", [], [P.ones_f], lambda e: e.memset(P.ones_f[:, :], 1.0))
        S.op("pool", [], [P.zeros_f], lambda e: e.memset(P.zeros_f[:, :], 0.0))
        S.op("pool", [P.ones_f], [P.ident_f], lambda e: e.affine_select(
            out=P.ident_f[:, :], in_=P.ones_f[:, :], pattern=[[-1, 128]], compare_op=ALU.is_equal,
            fill=0.0, base=0, channel_multiplier=1))
        S.op("pool", [P.ident_f], [P.ident_b], lambda e: e.tensor_copy(out=P.ident_b[:, :], in_=P.ident_f[:, :]))
        S.op("pool", [], [P.bones], lambda e: e.memset(P.bones[:, :], 0.0))
        S.op("pool", [], [P.bones], lambda e: e.memset(P.bones[0:64, 0:64], 1.0))
        S.op("pool", [], [P.bones], lambda e: e.memset(P.bones[64:128, 64:128], 1.0))
        ones3 = P.ones_f[0:64, 0:64].unsqueeze(1).to_broadcast([64, 4, 64])
        for m, cm, st, op in ((P.m_ge_f, -1, 1, ALU.is_ge), (P.m_gt_f, -1, 1, ALU.is_gt),
                              (P.m_ge_b, 1, -1, ALU.is_ge), (P.m_gt_b, 1, -1, ALU.is_gt)):
            S.op("pool", [P.ones_f], [m], lambda e, m=m, cm=cm, st=st, op=op: e.affine_select(
                out=m[0:64, :, :], in_=ones3, pattern=[[0, 4], [st, 64]], compare_op=op,
                fill=0.0, base=0, channel_multiplier=cm))
            S.op("dve", [m], [m], lambda e, m=m: e.tensor_copy(out=m[64:128, :, :], in_=m[0:64, :, :]))
        S.op("pool", [], [P.rmask], lambda e: e.memset(P.rmask[:, :], 1.0))
        for c in range(4):
            S.op("pool", [], [P.rmask], lambda e, c=c: e.memset(P.rmask[:, c * 64:c * 64 + 1], 0.0))
        for i, v in enumerate((EPS, 1e-12, 64e-5, 1.0)):
            S.op("pool", [], [P.eps], lambda e, i=i, v=v: e.memset(P.eps[:, i:i + 1], v))
        S.dma("sp", P.cfm[:, :, :], I.cfm, [], [P.cfm], P.cfm)
        S.op("act", [P.cfm], [P.scs], lambda e: e.activation(out=P.scs[:, :, :], in_=P.cfm[:, :, :], func=AF.Silu))

    consts()

    def phase_mod(l):
        st = ExitStack()
        wbuf = [sb(st, "adaw%d" % i, [128, 8, 512]) for i in range(2)]
        bfm = sb(st, "adab", [128, 48])
        npre = sb(st, "npre", [128, 8, 2])
        brow = sb(st, "brow", [128, 2, D])
        prow = sb(st, "prow", [128, 2, D])
        pm = psb(st, "pm", [128, 48, 2])
        P.G = sb(st, "Grow", [128, 2, 2, D])
        pr = [psb(st, "pr%d" % i, [128, 512]) for i in range(2)]
        screp = sb(st, "screp", [128, 8, 2, 128])
        S.op("dve", [P.scs], [screp], lambda e: e.tensor_copy(
            out=screp[:, :, :, :], in_=P.scs[:, :, :].unsqueeze(3).to_broadcast([128, 8, 2, 128])))
        S.dma("sp", bfm[:, :], I.adab_fm[l], [], [bfm], bfm)
        S.dma("sp", npre[:, :, :], I.npre_fm[l], [], [npre], npre)
        for sub in range(2):
            c0 = 2048 if sub == 0 else 5120
            S.dma("sp", brow[:, sub, :], I.ada_b[l, c0:c0 + D].partition_broadcast(128), [], [brow], brow)
            S.dma("sp", prow[:, sub, :], I.npost[l, sub, :].partition_broadcast(128), [], [prow], prow)
        awv = I.ada_w[l].rearrange("(dc p) n -> p dc n", p=128)
        rowblk = {4: (0, 0), 5: (0, 1), 10: (1, 0), 11: (1, 1)}
        k = 0
        for blk in range(12):
            wb = wbuf[blk % 2]
            S.dma("sp", wb[:, :, :], awv[:, :, blk * 512:(blk + 1) * 512], [], [wb], wb)
            for cc in range(4):
                col = blk * 4 + cc

                def f(e, wb=wb, cc=cc, col=col):
                    for dc in range(8):
                        ins = e.matmul(pm[:, col, :], wb[:, dc, cc * 128:(cc + 1) * 128], P.scs[:, dc, :],
                                       start=(dc == 0), stop=(dc == 7))
                    return ins
                S.op("pe", [wb, P.scs], [pm], f)
            if blk in rowblk:
                sub, half = rowblk[blk]
                for w in range(2):
                    pp = pr[k % 2]
                    k += 1

                    def f(e, wb=wb, w=w, pp=pp):
                        for dc in range(8):
                            ins = e.matmul(pp[:, :], screp[:, dc, w, :], wb[:, dc, :], start=(dc == 0), stop=(dc == 7))
                        return ins
                    S.op("pe", [wb, screp], [pp], f)
                    gs = P.G[:, w, sub, half * 512:(half + 1) * 512]
                    S.op("dve", [pp, brow], [P.G], lambda e, pp=pp, gs=gs, sub=sub, half=half: e.tensor_tensor(
                        out=gs, in0=pp[:, :], in1=brow[:, sub, half * 512:(half + 1) * 512], op=ALU.add))
                    S.op("dve", [prow], [P.G], lambda e, gs=gs, sub=sub, half=half: e.tensor_tensor(
                        out=gs, in0=gs, in1=prow[:, sub, half * 512:(half + 1) * 512], op=ALU.mult))
        S.op("dve", [pm, bfm], [P.modF], lambda e: e.tensor_tensor(
            out=P.modF[:, :, :], in0=pm[:, :, :], in1=bfm[:, :].unsqueeze(2).to_broadcast([128, 48, 2]), op=ALU.add))
        for sub in range(2):
            sh0 = 0 if sub == 0 else 24
            sc0 = 8 if sub == 0 else 32
            for w in range(2):
                S.op("dve", [P.modF, npre], [P.SB_], lambda e, sub=sub, w=w, sc0=sc0: e.scalar_tensor_tensor(
                    out=P.SB_[:, :, w, sub, 0], in0=P.modF[:, sc0:sc0 + 8, w], scalar=1.0, in1=npre[:, :, sub],
                    op0=ALU.add, op1=ALU.mult))
                S.op("dve", [P.modF], [P.SB_], lambda e, sub=sub, w=w, sh0=sh0: e.tensor_copy(
                    out=P.SB_[:, :, w, sub, 1], in_=P.modF[:, sh0:sh0 + 8, w]))
        S.dma("sp", X.Gd.t, P.G[:, :, :, :].rearrange("p a b d -> p (a b d)"), [P.G], [X.Gd], P.G)
        S.barrier()
        st.close()

    def row_pieces(buf_lat, buf_ctx, which, order, s0, n, width=None):
        if which == 1:
            return [(buf_ctx.t[s0:s0 + n, :], 0, n)]
        if order == "row":
            return [(buf_lat.t[s0:s0 + n, :], 0, n)]
        v = buf_lat.t.rearrange("(r c) d -> c r d", c=64)
        res = []
        for i in range(n // 64):
            c = s0 // 64 + i
            res.append((v[c], i * 64, 64))
        return res

    def scr_pieces(buf, which, order, s0, n):
        if which == 1:
            return [(buf.t[s0:s0 + n, :], 0, n)]
        if order == "row":
            return [(buf.t[TC + s0:TC + s0 + n, :], 0, n)]
        v = buf.t[TC:TC + TL, :].rearrange("(r c) d -> c r d", c=64)
        return [(v[s0 // 64 + i], i * 64, 64) for i in range(n // 64)]

    def make_hT(T, pieces, srcbufs, w, sub, hT, col0, keep_x=None):
        xt = keep_x if keep_x is not None else T.xt[T.k % 2]
        T.k += 1
        for (ap, po, m) in pieces:
            S.dma("sp", xt[po:po + m, :], ap, srcbufs, [xt], xt)
        S.op("act", [xt], [T.junk, T.ss], lambda e: e.activation(
            out=T.junk[:, :], in_=xt[:, :], func=AF.Square, accum_out=T.ss[:, 0:1]))
        S.op("act", [T.ss], [T.ss], lambda e: e.activation(
            out=T.ss[:, 1:2], in_=T.ss[:, 0:1], func=AF.Sqrt, bias=P.eps[:, 0:1], scale=1.0 / D))
        S.op("dve", [T.ss], [T.ss], lambda e: e.reciprocal(out=T.ss[:, 2:3], in_=T.ss[:, 1:2]))
        S.op("dve", [xt, T.ss], [T.xn], lambda e: e.tensor_scalar(
            out=T.xn[:, :], in0=xt[:, :], scalar1=T.ss[:, 2:3], scalar2=None, op0=ALU.mult))

        def f(e):
            for dc in range(8):
                ins = e.transpose(T.pt[:, dc, :], T.xn[:, dc * 128:(dc + 1) * 128], P.ident_b[:, :])
            return ins
        S.op("pe", [T.xn], [T.pt], f)
        for dc in range(8):
            S.op("act", [T.pt], [hT], lambda e, dc=dc: e.activation(
                out=hT[:, dc, col0:col0 + 128], in_=T.pt[:, dc, :], func=AF.Identity,
                scale=P.SB_[:, dc, w, sub, 0:1], bias=P.SB_[:, dc, w, sub, 1:2]))
        return xt

    def norm_tiles(st):
        T = Ctx()
        T.k = 0
        T.xt = [sb(st, "xt%d" % i, [128, D]) for i in range(2)]
        T.junk = sb(st, "junk", [128, D], BF16)
        T.ss = sb(st, "ss", [128, 4])
        T.xn = sb(st, "xn", [128, D], BF16)
        T.pt = psb(st, "ptr", [128, 8, 128], BF16)
        return T

    def load_w_bf16(st, name, src_ap, nchunk, ncols, R=()):
        wt = sb(st, name, [128, nchunk, ncols], BF16)
        v = src_ap.rearrange("(c p) n -> p c n", p=128)
        for c in range(nchunk):
            S.dma("pool", wt[:, c, :], v[:, c, :], [], [wt], wt)
        return wt

    TILES = [(1, 0)] + [(0, i * 256) for i in range(16)]

    def phase_A(l, src_lat, src_ctx, branch):
        st = ExitStack()
        order = ("row" if l % 2 == 0 else "col") if branch == "rw" else ("col" if l % 2 == 0 else "row")
        T = norm_tiles(st)
        if branch == "rw":
            Wt = load_w_bf16(st, "wA", I.w_in[l][:, 0:1920], 8, 1920)
            nch = 15
            raw = X.rawrw
        else:
            Wt = load_w_bf16(st, "wA", I.w_in[l][:, 1920:3488], 8, 1568)
            nch = 10
            raw = X.rawgl
        hT = [sb(st, "hT%d" % i, [128, 8, 256], BF16) for i in range(2)]
        stg = [sb(st, "stg%d" % i, [128, nch, 256]) for i in range(2)]
        pp = [psb(st, "ppA%d" % i, [128, 256]) for i in range(3)]
        if branch == "gla":
            gst = [sb(st, "gst%d" % i, [128, 512]) for i in range(2)]
            pg = [psb(st, "pgA%d" % i, [128, 512]) for i in range(2)]
        rawv = raw.t.rearrange("(c p) t -> p c t", p=128)
        for col in (0, 257, 258, 4355):
            S.dma("sp", rawv[:, :, col:col + 1], P.zeros_f[:, 0:nch].unsqueeze(2), [P.zeros_f], [raw], P.zeros_f,
                  allow_slow_non_contiguous=True)
        k = 0
        for ti, (w, s0) in enumerate(TILES[:ntlA]):
            h = hT[ti % 2]
            for sub in range(2):
                pcs = row_pieces(src_lat, src_ctx, w, order, s0 + sub * 128, 128)
                make_hT(T, pcs, [src_lat, src_ctx], w, 0, h, sub * 128)
            sg = stg[ti % 2]
            for cc in range(nch):
                if branch == "rw" or cc < 8:
                    cols = (cc * 128, 128)
                elif cc == 8:
                    cols = (1024, 16)
                else:
                    cols = (1040, 16)
                p_ = pp[k % 3]
                k += 1
                M = cols[1]

                def f(e, p_=p_, cols=cols, h=h, M=M):
                    for dc in range(8):
                        ins = e.matmul(p_[0:M, :], Wt[:, dc, cols[0]:cols[0] + M], h[:, dc, :], start=(dc == 0), stop=(dc == 7))
                    return ins
                S.op("pe", [Wt, h], [p_], f)
                eng = "act" if cc % 2 == 0 else "dve"
                if eng == "act":
                    S.op("act", [p_], [sg], lambda e, p_=p_, cc=cc, M=M: e.activation(out=sg[0:M, cc, :], in_=p_[0:M, :], func=AF.Copy))
                else:
                    S.op("dve", [p_], [sg], lambda e, p_=p_, cc=cc, M=M: e.tensor_copy(out=sg[0:M, cc, :], in_=p_[0:M, :]))
            off = (OFFC if w == 1 else OFFL) + s0
            if branch == "rw":
                S.dma("sp", rawv[:, :, off:off + 256], sg[:, :, :], [sg], [raw], sg)
            else:
                S.dma("sp", rawv[:, 0:8, off:off + 256], sg[:, 0:8, :], [sg], [raw], sg)
                S.dma("sp", rawv[0:16, 8:10, off:off + 256], sg[0:16, 8:10, :], [sg], [raw], sg)
                for sub in range(2):
                    g_ = gst[sub]
                    pgs = pg[sub]

                    def f(e, pgs=pgs, sub=sub, h=h):
                        for dc in range(8):
                            ins = e.matmul(pgs[:, :], h[:, dc, sub * 128:(sub + 1) * 128], Wt[:, dc, 1056:1568], start=(dc == 0), stop=(dc == 7))
                        return ins
                    S.op("pe", [Wt, h], [pgs], f)
                    S.op("act", [pgs], [g_], lambda e, pgs=pgs, g_=g_: e.activation(out=g_[:, :], in_=pgs[:, :], func=AF.Copy))
                    r0 = (0 if w == 1 else TC) + s0 + sub * 128
                    S.dma("sp", X.gatetm.t[r0:r0 + 128, :], g_[:, :], [g_], [X.gatetm], g_)
        S.barrier()
        st.close()

    def supertile_order(d):
        if d == 0:
            return [(1, 0)] + [(0, i * 256) for i in range(ntl - 1)]
        return [(1, 0)] + [(0, i * 256) for i in reversed(range(ntl - 1))]

    def hp(p):
        return slice(64 * p, 64 * p + 64)

    def r(ap):
        return ap.bitcast(F32R)

    class PsumPool2:
        def __init__(self, st, n):
            self.b = [psb(st, "pq%d" % i, [128, 1024]) for i in range(n)]
            self.k = 0

        def get(self):
            b = self.b[self.k % len(self.b)]
            self.k += 1
            return b

    def pcol(j, p, slot, wd=64):
        c0 = p * 512 + slot * 256 + j * wd
        return slice(c0, c0 + wd)

    def pview(ps, p, slot, nj=4, wd=64):
        c0 = p * 512 + slot * 256
        return ps[0:64, c0:c0 + nj * wd].rearrange("s (j v) -> s j v", v=wd)

    def evac(ps, slot, dstbuf, dst_fn, nj=4, wd=64, mask=None, engs=("act", "dve"), extraR=()):
        for p in range(2):
            src = pview(ps, p, slot, nj, wd)
            dst = dst_fn(p)
            if mask is not None:
                S.op("dve", [ps, mask] + list(extraR), [dstbuf], lambda e, src=src, dst=dst, p=p: e.tensor_tensor(
                    out=dst, in0=src, in1=mask[hp(p), 0:nj, :], op=ALU.mult))
            elif engs[p] == "act":
                S.op("act", [ps] + list(extraR), [dstbuf], lambda e, src=src, dst=dst: e.activation(out=dst, in_=src, func=AF.Copy))
            else:
                S.op("dve", [ps] + list(extraR), [dstbuf], lambda e, src=src, dst=dst: e.tensor_copy(out=dst, in_=src))

    def cum_block(T, SG, nj, d, c0):
        for j in range(nj):
            S.op("dve", [SG, P.rmask], [T.CS], lambda e, j=j: e.tensor_tensor_scan(
                out=T.CS[:, j, :], data0=P.rmask[:, :], data1=SG[:, j, :], initial=0.0, op0=ALU.mult, op1=ALU.add))
        csv = T.CS[:, 0:nj, :].rearrange("p j (c t) -> p j c t", t=64)
        S.op("dve", [T.CS], [T.TOT], lambda e: e.tensor_copy(out=T.TOT[:, 0:nj, :], in_=csv[:, :, :, 63]))
        if d == 1:
            S.op("dve", [T.CS, T.TOT], [T.CS], lambda e: e.tensor_tensor(
                out=csv, in0=T.TOT[:, 0:nj, :].unsqueeze(3).to_broadcast([128, nj, 4, 64]), in1=csv, op=ALU.subtract))
            S.op("dve", [SG], [T.CS], lambda e: e.tensor_tensor(
                out=T.CS[:, 0:nj, :], in0=T.CS[:, 0:nj, :], in1=SG[:, 0:nj, :], op=ALU.add))
        S.op("act", [T.CS], [T.E1], lambda e: e.activation(out=T.E1[:, 0:nj, :], in_=T.CS[:, 0:nj, :], func=AF.Exp, scale=-c0))
        S.op("act", [T.CS], [T.E2], lambda e: e.activation(out=T.E2[:, 0:nj, :], in_=T.CS[:, 0:nj, :], func=AF.Exp, scale=c0))
        S.op("act", [T.TOT], [T.Wc], lambda e: e.activation(out=T.Wc[:, 0:nj, :], in_=T.TOT[:, 0:nj, :], func=AF.Exp, scale=-c0))

    def run_phase(body, *a):
        st = ExitStack()
        try:
            body(*a, st)
        except StopPhase:
            pass
        S.barrier()
        st.close()

    def phase_B1(l, d):
        run_phase(phase_B1_body, l, d)

    def phase_B1_body(l, d, st):
        order = "row" if l % 2 == 0 else "col"
        T = Ctx()
        raw = sb(st, "raw0", [128, 15, 258])
        T1 = sb(st, "T1", [128, 15, 256])
        Fm = T1
        T.CS = sb(st, "CS", [128, 4, 256])
        T.TOT = sb(st, "TOT", [128, 4, 4])
        T.E1 = sb(st, "E1", [128, 4, 256])
        T.E2 = sb(st, "E2", [128, 4, 256])
        T.Wc = sb(st, "Wc", [128, 4, 4])
        SG = sb(st, "SG", [128, 4, 256])
        Aa = sb(st, "Aa", [128, 4, 256])
        kraw = sb(st, "kraw", [128, 4, 256])
        kkt = sb(st, "kkt", [128, 4, 256])
        tmp = sb(st, "tmp", [128, 4, 256])
        tmp2 = sb(st, "tmp2", [128, 4, 256])
        th = sb(st, "th", [128, 256])
        sgd = sb(st, "sgd", [128, 256])
        kt = kraw
        qh = sb(st, "qh", [128, 4, 256])
        kh = sb(st, "kh", [128, 4, 256])
        ah = sb(st, "ah", [128, 4, 256])
        bh = sb(st, "bh", [128, 4, 256])
        prod = sb(st, "prod", [128, 4, 256])
        vR = prod
        Vtm = sb(st, "Vtm", [128, 4, 4, 64])
        KhT = [sb(st, "KhT%d" % i, [128, 4, 64]) for i in range(2)]
        AhT = [sb(st, "AhT%d" % i, [128, 4, 64]) for i in range(2)]
        Gtm = sb(st, "Gtm", [128, 4, 4, 64])
        bon = sb(st, "bon", [128, 4, 4])
        Am = {nm: [sb(st, "%s%d" % (nm, c), [128, 4, 64]) for c in range(2)] for nm in ("AqkT", "AqaT", "AbkT", "TT")}
        Pq = [sb(st, "Pq%d" % i, [128, 4, 2, 64]) for i in range(2)]
        Rr = [sb(st, "Rr%d" % i, [128, 4, 64]) for i in range(2)]
        Zs = sb(st, "Zs", [128, 4, 64])
        Us = sb(st, "Us", [128, 4, 64])
        Hs = [sb(st, "Hs%d" % i, [128, 4, 64]) for i in range(2)]
        Ysb = sb(st, "Ysb", [128, 4, 4, 64])
        PS = PsumPool2(st, 3)
        pw = psb(st, "psw", [128, 1024])
        mu = sb(st, "mu", [128, 15])
        w0 = sb(st, "w0", [128, 4])
        a0 = sb(st, "a0", [128, 4])
        wup = sb(st, "wup", [128, 512])
        aup = sb(st, "aup", [128, 512])
        gup = sb(st, "gup", [128, 512])
        kkp = sb(st, "kkp", [128, 4, 3])
        omk = sb(st, "omk", [128, 4])
        gnw = sb(st, "gnw", [128, 2, 4, 64])
        S.dma("sp", mu[:, :], I.rw_mu_fm[l], [], [mu], mu)
        S.dma("sp", w0[:, :], I.rw_w0_fm[l, d], [], [w0], w0)
        S.dma("sp", a0[:, :], I.rw_a0_fm[l, d], [], [a0], a0)
        S.dma("sp", wup[:, :], I.rw_w_up[l], [], [wup], wup)
        S.dma("sp", aup[:, :], I.rw_a_up[l], [], [aup], aup)
        S.dma("sp", gup[:, :], I.rw_g_up[l], [], [gup], gup)
        S.dma("sp", kkp[:, :, :], I.kk_fm[l], [], [kkp], kkp)
        for i in range(2):
            gv = I.rw_gn[l, i, :].rearrange("(j p v) -> p j v", p=2, v=64)
            for p in range(2):
                S.dma("sp", gnw[hp(p), i, :, :], gv[p].partition_broadcast(64), [], [gnw], gnw)
        gupr = sb(st, "gupr", [128, 512])
        S.op("dve", [gup], [gupr], lambda e: e.tensor_copy(out=r(gupr[:, :]), in_=gup[:, :]))
        S.op("dve", [kkp], [omk], lambda e: e.tensor_scalar(out=omk[:, :], in0=kkp[:, :, 1], scalar1=-1.0, scalar2=1.0, op0=ALU.mult, op1=ALU.add))
        identr = sb(st, "identr", [128, 128])
        S.op("dve", [P.ident_f], [identr], lambda e: e.tensor_copy(out=r(identr[:, :]), in_=P.ident_f[:, :]))
        onesr = sb(st, "onesr", [128, 2])
        S.op("dve", [P.ones_f], [onesr], lambda e: e.tensor_copy(out=r(onesr[:, :]), in_=P.ones_f[:, 0:2]))
        S.op("dve", [P.zeros_f], [Hs[0]], lambda e: e.tensor_copy(out=r(Hs[0][:, :, :]), in_=P.zeros_f[:, 0:256].rearrange("p (j v) -> p j v", v=64)))
        if l == 1:
            vdn = sb(st, "vdn", [128, 4, 32])
            vup = sb(st, "vup", [32, 512])
            vbs = sb(st, "vbs", [128, 4, 64])
            vf = sb(st, "vf", [128, 4, 4, 64])
            zT = sb(st, "zT", [32, 256])
            gsb = sb(st, "gsb", [128, 4, 64])
            S.dma("sp", vdn[:, :, :], I.vres_down, [], [vdn], vdn)
            S.dma("sp", vup[:, :], I.vres_up, [], [vup], vup)
            bvv = I.vres_bias[0, :].rearrange("(j p v) -> p j v", p=2, v=64)
            for p in range(2):
                S.dma("sp", vbs[hp(p), :, :], bvv[p].partition_broadcast(64), [], [vbs], vbs)
        if d == 1:
            yfy = sb(st, "yfy", [128, 4, 4, 64])
            yfb = sb(st, "yfb", [128, 4, 4])
            stat = sb(st, "stat", [128, 16, 4])
        m_incl = P.m_ge_f if d == 0 else P.m_ge_b
        m_strict = P.m_gt_f if d == 0 else P.m_gt_b
        m_strictN = P.m_gt_b if d == 0 else P.m_gt_f
        rawv = X.rawrw.t.rearrange("(c p) t -> p c t", p=128)
        hstate = [0]

        def canon_rows(buf, w, s0, c):
            if w == 1:
                return buf.t[s0 + c * 64:s0 + (c + 1) * 64, :]
            if order == "row":
                return buf.t[TC + s0 + c * 64:TC + s0 + (c + 1) * 64, :]
            v = buf.t[TC:TC + TL, :].rearrange("(r c) d -> c r d", c=64)
            return v[s0 // 64 + c]

        for ti, (w, s0) in enumerate(supertile_order(d)):
            rw_ = raw
            off = (OFFC if w == 1 else OFFL) + s0
            S.dma("sp", rw_[:, :, :], rawv[:, :, off - 1:off + 257], [X.rawrw], [rw_], rw_)
            mid = rw_[:, :, 1:257]
            S.op("pool", [rw_], [T1], lambda e: e.tensor_tensor(out=T1[:, :, :], in0=rw_[:, :, 0:256], in1=rw_[:, :, 2:258], op=ALU.add))
            S.op("dve", [T1, rw_], [T1], lambda e: e.scalar_tensor_tensor(out=T1[:, :, :], in0=T1[:, :, :], scalar=0.5, in1=mid, op0=ALU.mult, op1=ALU.subtract))
            S.op("pool", [T1, mu], [T1], lambda e: e.tensor_tensor(out=T1[:, :, :], in0=T1[:, :, :], in1=mu[:, :].unsqueeze(2).to_broadcast([128, 15, 256]), op=ALU.mult))
            S.op("dve", [T1, rw_], [T1], lambda e: e.tensor_tensor(out=T1[:, :, :], in0=T1[:, :, :], in1=mid, op=ALU.add))
            Fr, Fk, Fv = Fm[:, 0:4, :], Fm[:, 4:8, :], Fm[:, 8:12, :]
            stage(1)
            S.op("pool", [Fm], [vR], lambda e: e.tensor_copy(out=r(vR[:, :, :]), in_=Fv))
            if l == 1:
                pz = PS.get()

                def f(e):
                    for j in range(4):
                        ins = e.matmul(pz[0:32, 0:256], vdn[:, j, :], Fm[:, 8 + j, :], start=(j == 0), stop=(j == 3))
                    return ins
                S.op("pe", [vdn, Fm], [pz], f)
                S.op("act", [pz], [zT], lambda e: e.activation(out=zT[:, :], in_=pz[0:32, 0:256], func=AF.Copy))
                for c in range(4):
                    vv = canon_rows(X.vfirst, w, s0, c).rearrange("t (j p v) -> p t j v", p=2, v=64)
                    for p in range(2):
                        S.dma("sp", vf[hp(p), c, :, :], vv[p], [X.vfirst], [vf], vf)
            for c in range(4):
                pv = PS.get()

                def f(e, c=c, pv=pv):
                    for p in range(2):
                        for j in range(4):
                            ins = e.matmul(pv[0:64, pcol(j, p, 0)], r(vR[hp(p), j, c * 64:(c + 1) * 64]),
                                           r(identr[hp(p), hp(p)]), start=True, stop=True)
                    return ins
                S.op("pe", [vR, identr], [pv], f)
                if l == 0:
                    evac(pv, 0, Vtm, lambda p, c=c: r(Vtm[hp(p), c, :, :]))
                else:
                    pg_ = pw
                    S.op("pe", [zT, vup], [pg_], lambda e, c=c: e.matmul(
                        pg_[0:64, 0:512], zT[:, c * 64:(c + 1) * 64], vup[:, :], start=True, stop=True))
                    for p in range(2):
                        gsrc = pg_[0:64, 0:512].rearrange("t (j p v) -> t p j v", p=2, v=64)[:, p, :, :]
                        S.op("dve", [pg_, vbs], [gsb], lambda e, p=p, gsrc=gsrc: e.tensor_tensor(out=gsb[hp(p), :, :], in0=gsrc, in1=vbs[hp(p), :, :], op=ALU.add))
                    S.op("act", [gsb], [gsb], lambda e: e.activation(out=gsb[:, :, :], in_=gsb[:, :, :], func=AF.Sigmoid))
                    for p in range(2):
                        pvv = pview(pv, p, 0)
                        S.op("dve", [vf, pv], [vf], lambda e, p=p, c=c, pvv=pvv: e.tensor_tensor(out=vf[hp(p), c, :, :], in0=vf[hp(p), c, :, :], in1=pvv, op=ALU.subtract))
                    S.op("pool", [vf, gsb], [vf], lambda e, c=c: e.tensor_tensor(out=vf[:, c, :, :], in0=vf[:, c, :, :], in1=gsb[:, :, :], op=ALU.mult))
                    for p in range(2):
                        pvv = pview(pv, p, 0)
                        S.op("dve", [vf, pv], [Vtm], lambda e, p=p, c=c, pvv=pvv: e.tensor_tensor(out=r(Vtm[hp(p), c, :, :]), in0=vf[hp(p), c, :, :], in1=pvv, op=ALU.add))
            if l == 0 and d == 0:
                for c in range(4):
                    vv = canon_rows(X.vfirst, w, s0, c).rearrange("t (j p v) -> p t j v", p=2, v=64)
                    for p in range(2):
                        S.dma("sp", vv[p], Vtm[hp(p), c, :, :], [Vtm], [X.vfirst], Vtm)
            stage(2)
            S.op("dve", [Fm, kkp], [kraw], lambda e: e.tensor_tensor(out=kraw[:, :, :], in0=Fk, in1=kkp[:, :, 0:1].to_broadcast([128, 4, 256]), op=ALU.mult))
            S.op("pool", [kraw], [tmp], lambda e: e.tensor_tensor(out=tmp[:, :, :], in0=kraw[:, :, :], in1=kraw[:, :, :], op=ALU.mult))

            def f(e):
                for hh in range(2):
                    ins = e.matmul(pw[:, hh * 512:(hh + 1) * 512], P.bones[:, :], tmp[:, 2 * hh:2 * hh + 2, :].rearrange("p a b -> p (a b)"), start=True, stop=True)
                return ins
            S.op("pe", [tmp], [pw], f)
            S.op("act", [pw], [tmp2], lambda e: e.activation(out=tmp2[:, :, :].rearrange("p a b -> p (a b)"), in_=pw[:, :], func=AF.Sqrt, bias=P.eps[:, 1:2], scale=1.0))
            S.op("dve", [tmp2], [tmp2], lambda e: e.reciprocal(out=tmp2[:, :, :], in_=tmp2[:, :, :]))
            S.op("dve", [kraw, tmp2], [kkt], lambda e: e.tensor_tensor(out=kkt[:, :, :], in0=kraw[:, :, :], in1=tmp2[:, :, :], op=ALU.mult))
            stage(3)
            hd = hp(d)
            S.op("act", [Fm], [th], lambda e: e.activation(out=th[hd, :], in_=Fm[hd, 12, :], func=AF.Tanh))

            def f(e):
                for j in range(4):
                    ins = e.matmul(pw[:, j * 256:(j + 1) * 256], wup[hd, j * 128:(j + 1) * 128], th[hd, :], start=True, stop=True)
                return ins
            S.op("pe", [wup, th], [pw], f)
            for j in range(4):
                S.op("act", [pw, w0], [SG], lambda e, j=j: e.activation(out=SG[:, j, :], in_=pw[:, j * 256:(j + 1) * 256], func=AF.Sigmoid, bias=w0[:, j:j + 1]))

            def f(e):
                for j in range(4):
                    ins = e.matmul(pw[:, j * 256:(j + 1) * 256], aup[hd, j * 128:(j + 1) * 128], Fm[hd, 13, :], start=True, stop=True)
                return ins
            S.op("pe", [aup, Fm], [pw], f)
            for j in range(4):
                S.op("act", [pw, a0], [Aa], lambda e, j=j: e.activation(out=Aa[:, j, :], in_=pw[:, j * 256:(j + 1) * 256], func=AF.Sigmoid, bias=a0[:, j:j + 1]))
            S.op("act", [Fm], [sgd], lambda e: e.activation(out=r(sgd[:, :]), in_=Fm[:, 14, :], func=AF.Sigmoid))
            stage(4)
            cum_block(T, SG, 4, d, C0_RW)
            S.op("dve", [T.CS, SG], [SG], lambda e: e.tensor_tensor(out=SG[:, :, :], in0=T.CS[:, :, :], in1=SG[:, :, :], op=ALU.subtract))
            S.op("act", [SG], [SG], lambda e: e.activation(out=SG[:, :, :], in_=SG[:, :, :], func=AF.Exp, scale=-C0_RW))
            E3 = SG
            stage(5)
            S.op("dve", [Fm, T.E1], [qh], lambda e: e.tensor_tensor(out=r(qh[:, :, :]), in0=Fr, in1=T.E1[:, :, :], op=ALU.mult))
            for j in range(4):
                S.op("dve", [Aa, kkp, omk], [tmp], lambda e, j=j: e.tensor_scalar(
                    out=tmp[:, j, :], in0=Aa[:, j, :], scalar1=kkp[:, j, 1:2], scalar2=omk[:, j:j + 1], op0=ALU.mult, op1=ALU.add))
            S.op("pool", [Fm, tmp], [kt], lambda e: e.tensor_tensor(out=kt[:, :, :], in0=Fk, in1=tmp[:, :, :], op=ALU.mult))
            S.op("pool", [Aa, kkt], [tmp2], lambda e: e.tensor_tensor(out=tmp2[:, :, :], in0=Aa[:, :, :], in1=kkt[:, :, :], op=ALU.mult))
            S.op("dve", [tmp2, T.E2], [ah], lambda e: e.scalar_tensor_tensor(out=r(ah[:, :, :]), in0=tmp2[:, :, :], scalar=-1.0, in1=T.E2[:, :, :], op0=ALU.mult, op1=ALU.mult))
            S.op("dve", [kt, T.E2], [kh], lambda e: e.tensor_tensor(out=r(kh[:, :, :]), in0=kt[:, :, :], in1=T.E2[:, :, :], op=ALU.mult))
            S.op("dve", [kkt, E3], [bh], lambda e: e.tensor_tensor(out=r(bh[:, :, :]), in0=kkt[:, :, :], in1=E3[:, :, :], op=ALU.mult))
            S.op("pool", [Fm, kkp], [tmp], lambda e: e.tensor_tensor(out=tmp[:, :, :], in0=Fr, in1=kkp[:, :, 2:3].to_broadcast([128, 4, 256]), op=ALU.mult))
            S.op("dve", [tmp, kt], [prod], lambda e: e.tensor_tensor(out=r(prod[:, :, :]), in0=tmp[:, :, :], in1=kt[:, :, :], op=ALU.mult))
            stage(6)

            def fm_(t_, j, p, cs):
                return r(t_[hp(p), j, cs])

            def heads():
                for p in range(2):
                    for j in range(4):
                        yield j, p

            def pre(c, sl):
                cs = slice(c * 64, (c + 1) * 64)
                pb = PS.get()

                def f(e):
                    for j, p in heads():
                        ins = e.matmul(pb[0:64, p * 512 + 2 * j:p * 512 + 2 * j + 2], fm_(prod, j, p, cs), r(onesr[hp(p), :]), start=True, stop=True)
                    return ins
                S.op("pe", [prod, onesr], [pb], f)
                for p in range(2):
                    S.op("act", [pb], [bon], lambda e, p=p: e.activation(
                        out=bon[hp(p), c, :], in_=pb[0:64, p * 512:p * 512 + 8].rearrange("s (j two) -> s j two", two=2)[:, :, 0], func=AF.Copy))
                pt_ = PS.get()

                def f(e):
                    for slot, src in ((0, kh), (1, ah)):
                        for j, p in heads():
                            ins = e.matmul(pt_[0:64, pcol(j, p, slot)], fm_(src, j, p, cs), r(identr[hp(p), hp(p)]), start=True, stop=True)
                    return ins
                S.op("pe", [kh, ah, identr], [pt_], f)
                evac(pt_, 0, KhT[sl], lambda p: r(KhT[sl][hp(p), :, :]), engs=("act", "act"))
                evac(pt_, 1, AhT[sl], lambda p: r(AhT[sl][hp(p), :, :]), engs=("act", "act"))
                pa1 = PS.get()

                def f(e):
                    for slot, lt, rt in ((0, kh, qh), (1, ah, qh)):
                        for j, p in heads():
                            ins = e.matmul(pa1[0:64, pcol(j, p, slot)], fm_(lt, j, p, cs), fm_(rt, j, p, cs), start=True, stop=True)
                    return ins
                S.op("pe", [kh, ah, qh], [pa1], f)
                evac(pa1, 0, Am["AqkT"][sl], lambda p: r(Am["AqkT"][sl][hp(p), :, :]), mask=m_incl)
                evac(pa1, 1, Am["AqaT"][sl], lambda p: r(Am["AqaT"][sl][hp(p), :, :]), mask=m_incl)
                pa2 = PS.get()

                def f(e):
                    for j, p in heads():
                        ins = e.matmul(pa2[0:64, pcol(j, p, 0)], fm_(kh, j, p, cs), fm_(bh, j, p, cs), start=True, stop=True)
                    return ins
                S.op("pe", [kh, bh], [pa2], f)
                evac(pa2, 0, Am["AbkT"][sl], lambda p: r(Am["AbkT"][sl][hp(p), :, :]), mask=m_strict)
                pa3 = PS.get()

                def f(e):
                    for slot, lt, rt in ((0, bh, ah), (1, ah, bh)):
                        for j, p in heads():
                            ins = e.matmul(pa3[0:64, pcol(j, p, slot)], fm_(lt, j, p, cs), fm_(rt, j, p, cs), start=True, stop=True)
                    return ins
                S.op("pe", [ah, bh], [pa3], f)
                pq = Pq[0]
                evac(pa3, 0, pq, lambda p: r(pq[hp(p), :, 0, :]), mask=m_strictN)
                evac(pa3, 1, pq, lambda p: r(pq[hp(p), :, 1, :]), mask=m_strict)
                rcur = Rr[0]
                for p in range(2):
                    S.op("pool", [pq, identr], [rcur], lambda e, p=p, rcur=rcur: e.tensor_tensor(
                        out=r(rcur[hp(p), :, :]), in0=pq[hp(p), :, 1, :], in1=identr[hp(p), hp(p)].unsqueeze(1).to_broadcast([64, 4, 64]), op=ALU.add))
                pcur = pq
                for lev in range(5):
                    pnx = Pq[(lev + 1) % 2]
                    pp_ = PS.get()

                    def f(e, pcur=pcur, pp_=pp_, lev=lev):
                        for j, p in heads():
                            ins = e.matmul(pp_[0:64, pcol(j, p, 0)], r(pcur[hp(p), j, 1, :]), r(pcur[hp(p), j, 0, :]), start=True, stop=True)
                        if lev < 4:
                            for j, p in heads():
                                ins = e.matmul(pp_[0:64, pcol(j, p, 1)], r(pcur[hp(p), j, 0, :]), r(pcur[hp(p), j, 1, :]), start=True, stop=True)
                        return ins
                    S.op("pe", [pcur], [pp_], f)
                    evac(pp_, 0, pnx, lambda p, pnx=pnx: r(pnx[hp(p), :, 0, :]), engs=("act", "act"))
                    if lev < 4:
                        evac(pp_, 1, pnx, lambda p, pnx=pnx: r(pnx[hp(p), :, 1, :]), engs=("dve", "dve"))
                    pr_ = PS.get()
                    rnx = Rr[(lev + 1) % 2] if lev < 4 else Am["TT"][sl]

                    def f(e, pnx=pnx, rcur=rcur, pr_=pr_):
                        for j, p in heads():
                            ins = e.matmul(pr_[0:64, pcol(j, p, 0)], r(pnx[hp(p), j, 0, :]), r(rcur[hp(p), j, :]), start=True, stop=True)
                        return ins
                    S.op("pe", [pnx, rcur], [pr_], f)
                    for p in range(2):
                        S.op("dve", [pr_, rcur], [rnx], lambda e, p=p, rnx=rnx, rcur=rcur, pr_=pr_: e.tensor_tensor(
                            out=r(rnx[hp(p), :, :]), in0=pview(pr_, p, 0), in1=rcur[hp(p), :, :], op=ALU.add))
                    rcur = rnx
                    pcur = pnx
                if d == 1:
                    S.op("pe", [sgd, gupr], [pw], lambda e: e.matmul(pw[0:64, 0:512], r(sgd[:, cs]), r(gupr[:, :]), start=True, stop=True))
                    for p in range(2):
                        gsrc = pw[0:64, 0:512].rearrange("t (j p v) -> t p j v", p=2, v=64)[:, p, :, :]
                        S.op("act", [pw], [Gtm], lambda e, p=p, gsrc=gsrc: e.activation(out=Gtm[hp(p), c, :, :], in_=gsrc, func=AF.Copy))

            def seq(c, sl):
                cs = slice(c * 64, (c + 1) * 64)
                Hc, Hn = Hs[hstate[0]], Hs[1 - hstate[0]]
                pz = PS.get()

                def f(e):
                    for j, p in heads():
                        e.matmul(pz[0:64, pcol(j, p, 0)], r(Am["AbkT"][sl][hp(p), j, :]), r(Vtm[hp(p), c, j, :]), start=True, stop=False)
                        ins = e.matmul(pz[0:64, pcol(j, p, 0)], fm_(bh, j, p, cs), r(Hc[hp(p), j, :]), start=False, stop=True)
                    return ins
                S.op("pe", [Am["AbkT"][sl], Vtm, bh, Hc], [pz], f)
                evac(pz, 0, Zs, lambda p: r(Zs[hp(p), :, :]), engs=("act", "dve"))
                pu = PS.get()

                def f(e):
                    for j, p in heads():
                        ins = e.matmul(pu[0:64, pcol(j, p, 0)], r(Am["TT"][sl][hp(p), j, :]), r(Zs[hp(p), j, :]), start=True, stop=True)
                    return ins
                S.op("pe", [Am["TT"][sl], Zs], [pu], f)
                evac(pu, 0, Us, lambda p: r(Us[hp(p), :, :]), engs=("act", "dve"))
                ph = PS.get()

                def f(e):
                    for j, p in heads():
                        oc = ph[0:64, pcol(j, p, 0)]
                        e.matmul(oc, r(identr[hp(p), hp(p)]), r(Hc[hp(p), j, :]), start=True, stop=False)
                        e.matmul(oc, r(KhT[sl][hp(p), j, :]), r(Vtm[hp(p), c, j, :]), start=False, stop=False)
                        ins = e.matmul(oc, r(AhT[sl][hp(p), j, :]), r(Us[hp(p), j, :]), start=False, stop=True)
                    return ins
                S.op("pe", [identr, Hc, KhT[sl], Vtm, AhT[sl], Us], [ph], f)
                for p in range(2):
                    S.op("dve", [ph, T.Wc], [Hn], lambda e, p=p: e.tensor_tensor(
                        out=r(Hn[hp(p), :, :]), in0=pview(ph, p, 0),
                        in1=T.Wc[hp(p), :, c:c + 1].to_broadcast([64, 4, 64]), op=ALU.mult))

                def f(e):
                    for j, p in heads():
                        oc = ph[0:64, pcol(j, p, 1)]
                        e.matmul(oc, fm_(qh, j, p, cs), r(Hc[hp(p), j, :]), start=True, stop=False)
                        e.matmul(oc, r(Am["AqkT"][sl][hp(p), j, :]), r(Vtm[hp(p), c, j, :]), start=False, stop=False)
                        ins = e.matmul(oc, r(Am["AqaT"][sl][hp(p), j, :]), r(Us[hp(p), j, :]), start=False, stop=True)
                    return ins
                S.op("pe", [qh, Hc, Am["AqkT"][sl], Vtm, Am["AqaT"][sl], Us], [ph], f)
                evac(ph, 1, Ysb, lambda p: Ysb[hp(p), c, :, :], engs=("act", "act"))
                hstate[0] = 1 - hstate[0]
            co = list(range(4)) if d == 0 else [3, 2, 1, 0]
            pre(co[0], 0)
            stage(7)
            pre(co[1], 1)
            seq(co[0], 0)
            stage(8)
            pre(co[2], 0)
            seq(co[1], 1)
            pre(co[3], 1)
            seq(co[2], 0)
            seq(co[3], 1)
            r0 = (0 if w == 1 else TC) + s0
            if d == 0:
                for p in range(2):
                    S.dma("sp", X.yrwf.t[p, r0:r0 + 256, :].rearrange("(c t) n -> t c n", t=64),
                          Ysb[hp(p), :, :, :].rearrange("s c j v -> s c (j v)"), [Ysb], [X.yrwf], Ysb)
                    S.dma("sp", X.yrwb.t[p, r0:r0 + 256, :].rearrange("(c t) n -> t c n", t=64), bon[hp(p), :, :], [bon], [X.yrwb], bon)
            else:
                for p in range(2):
                    S.dma("sp", yfy[hp(p), :, :, :].rearrange("s c j v -> s c (j v)"),
                          X.yrwf.t[p, r0:r0 + 256, :].rearrange("(c t) n -> t c n", t=64), [X.yrwf], [yfy], yfy)
                    S.dma("sp", yfb[hp(p), :, :], X.yrwb.t[p, r0:r0 + 256, :].rearrange("(c t) n -> t c n", t=64), [X.yrwb], [yfb], yfb)
                y3 = Ysb[:, :, :, :].rearrange("s c j v -> s (c j) v")
                S.op("dve", [yfy, Ysb], [Ysb], lambda e: e.tensor_tensor(
                    out=y3, in0=y3, in1=yfy[:, :, :, :].rearrange("s c j v -> s (c j) v"), op=ALU.add))
                S.op("dve", [Ysb], [stat], lambda e: e.tensor_reduce(out=stat[:, :, 0], in_=y3, axis=AX.X, op=ALU.add))
                S.op("dve", [stat], [stat], lambda e: e.tensor_scalar(out=stat[:, :, 0], in0=stat[:, :, 0], scalar1=1.0 / 64, scalar2=None, op0=ALU.mult))
                S.op("dve", [Ysb, stat], [Ysb], lambda e: e.tensor_tensor(
                    out=y3, in0=y3, in1=stat[:, :, 0:1].to_broadcast([128, 16, 64]), op=ALU.subtract))
                ysq = T1[:, 0:4, :].rearrange("s a (b v) -> s (a b) v", v=64)
                S.op("pool", [Ysb], [T1], lambda e: e.tensor_tensor(out=ysq, in0=y3, in1=y3, op=ALU.mult))
                S.op("dve", [T1], [stat], lambda e: e.tensor_reduce(out=stat[:, :, 1], in_=ysq, axis=AX.X, op=ALU.add))
                S.op("act", [stat], [stat], lambda e: e.activation(out=stat[:, :, 2], in_=stat[:, :, 1], func=AF.Sqrt, bias=P.eps[:, 2:3], scale=1.0 / 64))
                S.op("dve", [stat], [stat], lambda e: e.reciprocal(out=stat[:, :, 3], in_=stat[:, :, 2]))
                S.op("dve", [Ysb, stat], [Ysb], lambda e: e.tensor_tensor(
                    out=y3, in0=y3, in1=stat[:, :, 3:4].to_broadcast([128, 16, 64]), op=ALU.mult))
                for i, op in ((0, ALU.mult), (1, ALU.add)):
                    S.op("dve", [Ysb, gnw], [Ysb], lambda e, i=i, op=op: e.tensor_tensor(
                        out=Ysb[:, :, :, :], in0=Ysb[:, :, :, :], in1=gnw[:, i, :, :].unsqueeze(1).to_broadcast([128, 4, 4, 64]), op=op))
                S.op("dve", [yfb, bon], [stat], lambda e: e.tensor_tensor(
                    out=stat[:, :, 0], in0=yfb[:, :, :].rearrange("s c j -> s (c j)"), in1=bon[:, :, :].rearrange("s c j -> s (c j)"), op=ALU.add))
                S.op("pool", [Vtm, stat], [T1], lambda e: e.tensor_tensor(
                    out=ysq, in0=Vtm[:, :, :, :].rearrange("s c j v -> s (c j) v"), in1=stat[:, :, 0:1].to_broadcast([128, 16, 64]), op=ALU.mult))
                S.op("dve", [Ysb, T1], [Ysb], lambda e: e.tensor_tensor(out=y3, in0=y3, in1=ysq, op=ALU.add))
                S.op("dve", [Ysb, Gtm], [Gtm], lambda e: e.tensor_tensor(out=Gtm[:, :, :, :], in0=Ysb[:, :, :, :], in1=Gtm[:, :, :, :], op=ALU.mult))
                for c in range(4):
                    vv = canon_rows(X.yrw, w, s0, c).rearrange("t (j p v) -> p t j v", p=2, v=64)
                    for p in range(2):
                        S.dma("sp", vv[p], Gtm[hp(p), c, :, :], [Gtm], [X.yrw], Gtm)

    def phase_B2(l, d):
        run_phase(phase_B2_body, l, d)

    def phase_B2_body(l, d, st):
        order = "col" if l % 2 == 0 else "row"
        T = Ctx()
        raw = sb(st, "graw", [128, 10, 258])
        T1 = sb(st, "gT1", [128, 8, 256])
        T2 = sb(st, "gT2", [128, 8, 256])
        Fm = sb(st, "gFm", [128, 8, 256])
        T.CS = sb(st, "gCS", [128, 2, 256])
        T.TOT = sb(st, "gTOT", [128, 2, 4])
        T.E1 = sb(st, "gE1", [128, 2, 256])
        T.E2 = sb(st, "gE2", [128, 2, 256])
        T.Wc = sb(st, "gWc", [128, 2, 4])
        SG = sb(st, "gSG", [128, 2, 256])
        qh = sb(st, "gqh", [128, 2, 256])
        kh = sb(st, "gkh", [128, 2, 256])
        vR = sb(st, "gvR", [128, 4, 256])
        Vtm = sb(st, "gVtm", [128, 4, 2, 128])
        KhT = [sb(st, "gKhT%d" % i, [128, 2, 64]) for i in range(2)]
        AqkT = [sb(st, "gAqk%d" % i, [128, 2, 64]) for i in range(2)]
        Hs = [sb(st, "gHs%d" % i, [128, 2, 128]) for i in range(2)]
        Ysb = sb(st, "gYsb", [128, 4, 2, 128])
        PS = PsumPool2(st, 3)
        pw = psb(st, "gpw", [128, 512])
        cw = sb(st, "gcw", [128, 8, 3])
        aup = sb(st, "gaup", [16, 256])
        ab = sb(st, "gab", [128, 2])
        nab = sb(st, "gnab", [128, 2])
        S.dma("sp", cw[:, :, :], I.gla_conv_fm[l], [], [cw], cw)
        S.dma("sp", aup[:, :], I.gla_alpha_up[l, d], [], [aup], aup)
        S.dma("sp", ab[:, :], I.gla_abias_fm[l, d], [], [ab], ab)
        S.op("dve", [ab], [nab], lambda e: e.tensor_scalar(out=nab[:, :], in0=ab[:, :], scalar1=-1.0, scalar2=None, op0=ALU.mult))
        identr = sb(st, "gidentr", [128, 128])
        S.op("dve", [P.ident_f], [identr], lambda e: e.tensor_copy(out=r(identr[:, :]), in_=P.ident_f[:, :]))
        S.op("dve", [P.zeros_f], [Hs[0]], lambda e: e.tensor_copy(out=r(Hs[0][:, :, :]), in_=P.zeros_f[:, 0:256].rearrange("p (j v) -> p j v", v=128)))
        if d == 1:
            yf = sb(st, "gyf", [128, 4, 2, 128])
            gt = sb(st, "ggt", [128, 4, 2, 128])
            osq = sb(st, "gosq", [128, 8, 128])
            stat = sb(st, "gstat", [128, 8, 4])
            nw = sb(st, "gnw", [128, 128])
            S.dma("sp", nw[:, :], I.gla_norm_w[l, :].partition_broadcast(128), [], [nw], nw)
        m_incl = P.m_ge_f if d == 0 else P.m_ge_b
        rawv = X.rawgl.t.rearrange("(c p) t -> p c t", p=128)
        hstate = [0]

        def canon_rows(buf, w, s0, c):
            if w == 1:
                return buf.t[s0 + c * 64:s0 + (c + 1) * 64, :]
            if order == "row":
                return buf.t[TC + s0 + c * 64:TC + s0 + (c + 1) * 64, :]
            v = buf.t[TC:TC + TL, :].rearrange("(r c) d -> c r d", c=64)
            return v[s0 // 64 + c]

        def heads():
            for p in range(2):
                for j in range(2):
                    yield j, p
        for ti, (w, s0) in enumerate(supertile_order(d)):
            rw_ = raw
            off = (OFFC if w == 1 else OFFL) + s0
            r0 = (0 if w == 1 else TC) + s0
            S.dma("sp", rw_[:, 0:8, :], rawv[:, 0:8, off - 1:off + 257], [X.rawgl], [rw_], rw_)
            S.dma("sp", rw_[0:16, 8:10, :], rawv[0:16, 8:10, off - 1:off + 257], [X.rawgl], [rw_], rw_)

            def cwb(i):
                return cw[:, :, i:i + 1].to_broadcast([128, 8, 256])
            S.op("pool", [rw_, cw], [T1], lambda e: e.tensor_tensor(out=T1[:, :, :], in0=rw_[:, 0:8, 0:256], in1=cwb(0), op=ALU.mult))
            S.op("dve", [rw_, cw], [T2], lambda e: e.tensor_tensor(out=T2[:, :, :], in0=rw_[:, 0:8, 1:257], in1=cwb(1), op=ALU.mult))
            S.op("pool", [T1, T2], [T1], lambda e: e.tensor_tensor(out=T1[:, :, :], in0=T1[:, :, :], in1=T2[:, :, :], op=ALU.add))
            S.op("dve", [rw_, cw], [T2], lambda e: e.tensor_tensor(out=T2[:, :, :], in0=rw_[:, 0:8, 2:258], in1=cwb(2), op=ALU.mult))
            S.op("pool", [T1, T2], [T1], lambda e: e.tensor_tensor(out=T1[:, :, :], in0=T1[:, :, :], in1=T2[:, :, :], op=ALU.add))
            S.op("act", [T1], [Fm], lambda e: e.activation(out=Fm[:, :, :], in_=T1[:, :, :], func=AF.Silu))

            def f(e):
                for jj in range(2):
                    ins = e.matmul(pw[:, jj * 256:(jj + 1) * 256], aup[:, jj * 128:(jj + 1) * 128], rw_[0:16, 8 + d, 1:257], start=True, stop=True)
                return ins
            S.op("pe", [aup, rw_], [pw], f)
            for jj in range(2):
                S.op("act", [pw, nab], [SG], lambda e, jj=jj: e.activation(out=SG[:, jj, :], in_=pw[:, jj * 256:(jj + 1) * 256], func=AF.Exp, bias=nab[:, jj:jj + 1], scale=-1.0))
            S.op("act", [SG], [SG], lambda e: e.activation(out=SG[:, :, :], in_=SG[:, :, :], func=AF.Ln, bias=P.eps[:, 3:4], scale=1.0))
            cum_block(T, SG, 2, d, C0_GLA)
            S.op("dve", [Fm, T.E1], [qh], lambda e: e.scalar_tensor_tensor(out=r(qh[:, :, :]), in0=Fm[:, 0:2, :], scalar=0.125, in1=T.E1[:, :, :], op0=ALU.mult, op1=ALU.mult))
            S.op("dve", [Fm, T.E2], [kh], lambda e: e.tensor_tensor(out=r(kh[:, :, :]), in0=Fm[:, 2:4, :], in1=T.E2[:, :, :], op=ALU.mult))
            S.op("pool", [Fm], [vR], lambda e: e.tensor_copy(out=r(vR[:, :, :]), in_=Fm[:, 4:8, :]))
            if d == 1:
                for p in range(2):
                    S.dma("sp", yf[hp(p), :, :, :].rearrange("s c j v -> s c (j v)"),
                          X.yglf.t[p, r0:r0 + 256, :].rearrange("(c t) n -> t c n", t=64), [X.yglf], [yf], yf)
                    gv = X.gatetm.t[r0:r0 + 256, :].rearrange("(c t) (j p v) -> p t c j v", t=64, p=2, v=128)
                    for c in range(4):
                        S.dma("sp", gt[hp(p), c, :, :], gv[p][:, c, :, :], [X.gatetm], [gt], gt)

            def pre(c, sl):
                cs = slice(c * 64, (c + 1) * 64)
                pv = PS.get()

                def f(e):
                    for j, p in heads():
                        ins = e.matmul(pv[0:64, p * 512 + j * 128:p * 512 + (j + 1) * 128], r(vR[:, 2 * j + p, cs]), r(identr[:, :]), start=True, stop=True)
                    return ins
                S.op("pe", [vR, identr], [pv], f)
                for p in range(2):
                    S.op("act", [pv], [Vtm], lambda e, p=p: e.activation(
                        out=r(Vtm[hp(p), c, :, :]), in_=pv[0:64, p * 512:p * 512 + 256].rearrange("s (j v) -> s j v", v=128), func=AF.Copy))
                pt_ = PS.get()

                def f(e):
                    for j, p in heads():
                        ins = e.matmul(pt_[0:64, pcol(j, p, 0)], r(kh[hp(p), j, cs]), r(identr[hp(p), hp(p)]), start=True, stop=True)
                    for j, p in heads():
                        ins = e.matmul(pt_[0:64, pcol(j, p, 1)], r(kh[hp(p), j, cs]), r(qh[hp(p), j, cs]), start=True, stop=True)
                    return ins
                S.op("pe", [kh, qh, identr], [pt_], f)
                evac(pt_, 0, KhT[sl], lambda p: r(KhT[sl][hp(p), :, :]), nj=2, engs=("dve", "dve"))
                evac(pt_, 1, AqkT[sl], lambda p: r(AqkT[sl][hp(p), :, :]), nj=2, mask=m_incl)

            def seq(c, sl):
                cs = slice(c * 64, (c + 1) * 64)
                Hc, Hn = Hs[hstate[0]], Hs[1 - hstate[0]]
                ph = PS.get()

                def f(e):
                    for j, p in heads():
                        oc = ph[0:64, p * 512 + j * 128:p * 512 + (j + 1) * 128]
                        e.matmul(oc, r(identr[hp(p), hp(p)]), r(Hc[hp(p), j, :]), start=True, stop=False)
                        ins = e.matmul(oc, r(KhT[sl][hp(p), j, :]), r(Vtm[hp(p), c, j, :]), start=False, stop=True)
                    return ins
                S.op("pe", [identr, Hc, KhT[sl], Vtm], [ph], f)
                for p in range(2):
                    S.op("dve", [ph, T.Wc], [Hn], lambda e, p=p: e.tensor_tensor(
                        out=r(Hn[hp(p), :, :]), in0=ph[0:64, p * 512:p * 512 + 256].rearrange("s (j v) -> s j v", v=128),
                        in1=T.Wc[hp(p), :, c:c + 1].to_broadcast([64, 2, 128]), op=ALU.mult))
                py = PS.get()

                def f(e):
                    for j, p in heads():
                        oc = py[0:64, p * 512 + j * 128:p * 512 + (j + 1) * 128]
                        e.matmul(oc, r(qh[hp(p), j, cs]), r(Hc[hp(p), j, :]), start=True, stop=False)
                        ins = e.matmul(oc, r(AqkT[sl][hp(p), j, :]), r(Vtm[hp(p), c, j, :]), start=False, stop=True)
                    return ins
                S.op("pe", [qh, Hc, AqkT[sl], Vtm], [py], f)
                for p in range(2):
                    S.op("act", [py], [Ysb], lambda e, p=p: e.activation(
                        out=Ysb[hp(p), c, :, :], in_=py[0:64, p * 512:p * 512 + 256].rearrange("s (j v) -> s j v", v=128), func=AF.Copy))
                hstate[0] = 1 - hstate[0]
            co = list(range(4)) if d == 0 else [3, 2, 1, 0]
            pre(co[0], 0)
            pre(co[1], 1)
            seq(co[0], 0)
            pre(co[2], 0)
            seq(co[1], 1)
            pre(co[3], 1)
            seq(co[2], 0)
            seq(co[3], 1)
            if d == 0:
                for p in range(2):
                    S.dma("sp", X.yglf.t[p, r0:r0 + 256, :].rearrange("(c t) n -> t c n", t=64),
                          Ysb[hp(p), :, :, :].rearrange("s c j v -> s c (j v)"), [Ysb], [X.yglf], Ysb)
            else:
                o3 = Ysb[:, :, :, :].rearrange("s c j v -> s (c j) v")
                S.op("dve", [yf, Ysb], [Ysb], lambda e: e.tensor_tensor(out=o3, in0=o3, in1=yf[:, :, :, :].rearrange("s c j v -> s (c j) v"), op=ALU.add))
                S.op("pool", [Ysb], [osq], lambda e: e.tensor_tensor(out=osq[:, :, :], in0=o3, in1=o3, op=ALU.mult))
                S.op("dve", [osq], [stat], lambda e: e.tensor_reduce(out=stat[:, :, 0], in_=osq[:, :, :], axis=AX.X, op=ALU.add))
                S.op("act", [stat], [stat], lambda e: e.activation(out=stat[:, :, 1], in_=stat[:, :, 0], func=AF.Sqrt, bias=P.eps[:, 0:1], scale=1.0 / 128))
                S.op("dve", [stat], [stat], lambda e: e.reciprocal(out=stat[:, :, 2], in_=stat[:, :, 1]))
                S.op("dve", [Ysb, stat], [Ysb], lambda e: e.tensor_tensor(out=o3, in0=o3, in1=stat[:, :, 2:3].to_broadcast([128, 8, 128]), op=ALU.mult))
                S.op("dve", [Ysb, nw], [Ysb], lambda e: e.tensor_tensor(out=o3, in0=o3, in1=nw[:, :].unsqueeze(1).to_broadcast([128, 8, 128]), op=ALU.mult))
                S.op("act", [gt], [yf], lambda e: e.activation(out=yf[:, :, :, :], in_=gt[:, :, :, :], func=AF.Silu))
                S.op("dve", [Ysb, yf], [yf], lambda e: e.tensor_tensor(out=yf[:, :, :, :], in0=Ysb[:, :, :, :], in1=yf[:, :, :, :], op=ALU.mult))
                for c in range(4):
                    vv = canon_rows(X.ygl, w, s0, c).rearrange("t (j p v) -> p t j v", p=2, v=128)
                    for p in range(2):
                        S.dma("sp", vv[p], yf[hp(p), c, :, :], [yf], [X.ygl], yf)

    def rms_residual(T, pm2, xt, w, sub, dst_ap, dstbuf, res):
        for i in range(2):
            S.op("act", [pm2[i]], [T.junk, T.ss2], lambda e, i=i: e.activation(
                out=T.junk[:, 0:512], in_=pm2[i][:, :], func=AF.Square, accum_out=T.ss2[:, i:i + 1]))
        S.op("dve", [T.ss2], [T.ss2], lambda e: e.tensor_tensor(out=T.ss2[:, 2:3], in0=T.ss2[:, 0:1], in1=T.ss2[:, 1:2], op=ALU.add))
        S.op("act", [T.ss2], [T.ss2], lambda e: e.activation(out=T.ss2[:, 3:4], in_=T.ss2[:, 2:3], func=AF.Sqrt, bias=P.eps[:, 0:1], scale=1.0 / D))
        S.op("dve", [T.ss2], [T.ss2], lambda e: e.reciprocal(out=T.ss2[:, 2:3], in_=T.ss2[:, 3:4]))
        for i in range(2):
            hs = slice(i * 512, (i + 1) * 512)
            S.op("dve", [pm2[i], T.ss2, P.G], [res], lambda e, i=i, hs=hs: e.scalar_tensor_tensor(
                out=res[:, hs], in0=pm2[i][:, :], scalar=T.ss2[:, 2:3], in1=P.G[:, w, sub, hs], op0=ALU.mult, op1=ALU.mult))
        S.op("pool", [res, xt], [res], lambda e: e.tensor_tensor(out=res[:, :], in0=res[:, :], in1=xt[:, :], op=ALU.add))
        S.dma("sp", dst_ap, res[:, :], [res], [dstbuf], res)

    def phase_C1(l, src_lat, src_ctx, dst_lat, dst_ctx, do_ctx):
        st = ExitStack()
        T = norm_tiles(st)
        T.ss2 = sb(st, "ss2", [128, 4])
        P.G = sb(st, "Grow", [128, 2, 2, D])
        S.dma("sp", P.G[:, :, :, :].rearrange("p a b d -> p (a b d)"), X.Gd.t, [X.Gd], [P.G], P.G)
        Wg = load_w_bf16(st, "wg", I.w_in[l][:, 3488:5536], 8, 2048)
        Wr = load_w_bf16(st, "wr", I.rw_out[l], 4, D)
        Wl = load_w_bf16(st, "wl", I.gla_out[l], 4, D)
        Wm = load_w_bf16(st, "wm", I.merge_out[l], 8, D)
        hT = sb(st, "c1hT", [128, 8, 128], BF16)
        sg = sb(st, "c1sg", [128, 2048])
        yin = [sb(st, "c1y%d" % i, [128, 512]) for i in range(2)]
        yb = sb(st, "c1yb", [128, 512], BF16)
        yT = [sb(st, "c1yT%d" % i, [128, 4, 128], BF16) for i in range(2)]
        Xs = sb(st, "c1X", [128, D])
        Xb = sb(st, "c1Xb", [128, D], BF16)
        XT = sb(st, "c1XT", [128, 8, 128], BF16)
        res = sb(st, "c1res", [128, D])
        pb = [psb(st, "c1p%d" % i, [128, 512]) for i in range(6)]
        ntile = (2 if do_ctx else 0) + 32
        for ti in range(min(ntile, 2 * ntlC)):
            if do_ctx and ti < 2:
                w, t0, sbuf_, dbuf_ = 1, ti * 128, src_ctx, dst_ctx
                yr0 = t0
            else:
                tt = ti - (2 if do_ctx else 0)
                w, t0, sbuf_, dbuf_ = 0, tt * 128, src_lat, dst_lat
                yr0 = TC + t0
            xt = make_hT(T, [(sbuf_.t[t0:t0 + 128, :], 0, 128)], [sbuf_], w, 0, hT, 0)
            for n in range(4):
                pg = pb[n]

                def f(e, pg=pg, n=n):
                    for dc in range(8):
                        ins = e.matmul(pg[:, :], hT[:, dc, :], Wg[:, dc, n * 512:(n + 1) * 512], start=(dc == 0), stop=(dc == 7))
                    return ins
                S.op("pe", [hT, Wg], [pg], f)
                S.op("act", [pg], [sg], lambda e, pg=pg, n=n: e.activation(out=sg[:, n * 512:(n + 1) * 512], in_=pg[:, :], func=AF.Sigmoid))
            for bi, (ybuf, Wo) in enumerate(((X.yrw, Wr), (X.ygl, Wl))):
                yi = yin[bi]
                S.dma("sp", yi[:, :], ybuf.t[yr0:yr0 + 128, :], [ybuf], [yi], yi)
                S.op("dve", [yi], [yb], lambda e, yi=yi: e.tensor_copy(out=yb[:, :], in_=yi[:, :]))

                def f(e):
                    for c4 in range(4):
                        ins = e.transpose(T.pt[:, c4, :], yb[:, c4 * 128:(c4 + 1) * 128], P.ident_b[:, :])
                    return ins
                S.op("pe", [yb], [T.pt], f)
                S.op("act", [T.pt], [yT[bi]], lambda e, bi=bi: e.activation(out=yT[bi][:, :, :], in_=T.pt[:, 0:4, :], func=AF.Copy))
                for n in range(2):
                    pp_ = pb[4 + n]

                    def f(e, pp_=pp_, n=n, bi=bi, Wo=Wo):
                        for c4 in range(4):
                            ins = e.matmul(pp_[:, :], yT[bi][:, c4, :], Wo[:, c4, n * 512:(n + 1) * 512], start=(c4 == 0), stop=(c4 == 3))
                        return ins
                    S.op("pe", [yT[bi], Wo], [pp_], f)
                    hs = slice(n * 512, (n + 1) * 512)
                    gs = slice(bi * 1024 + n * 512, bi * 1024 + (n + 1) * 512)
                    if bi == 0:
                        S.op("dve", [pp_, sg], [Xs], lambda e, pp_=pp_, hs=hs, gs=gs: e.tensor_tensor(out=Xs[:, hs], in0=pp_[:, :], in1=sg[:, gs], op=ALU.mult))
                    else:
                        S.op("dve", [pp_, sg], [sg], lambda e, pp_=pp_, gs=gs: e.tensor_tensor(out=sg[:, gs], in0=pp_[:, :], in1=sg[:, gs], op=ALU.mult))
                        S.op("pool", [Xs, sg], [Xb], lambda e, hs=hs, gs=gs: e.tensor_tensor(out=Xb[:, hs], in0=Xs[:, hs], in1=sg[:, gs], op=ALU.add))

            def f(e):
                for dc in range(8):
                    ins = e.transpose(T.pt[:, dc, :], Xb[:, dc * 128:(dc + 1) * 128], P.ident_b[:, :])
                return ins
            S.op("pe", [Xb], [T.pt], f)
            S.op("act", [T.pt], [XT], lambda e: e.activation(out=XT[:, :, :], in_=T.pt[:, :, :], func=AF.Copy))
            pm2 = [pb[0], pb[1]]
            for n in range(2):
                def f(e, n=n):
                    for dc in range(8):
                        ins = e.matmul(pm2[n][:, :], XT[:, dc, :], Wm[:, dc, n * 512:(n + 1) * 512], start=(dc == 0), stop=(dc == 7))
                    return ins
                S.op("pe", [XT, Wm], [pm2[n]], f)
            rms_residual(T, pm2, xt, w, 0, dbuf_.t[t0:t0 + 128, :], dbuf_, res)
        S.barrier()
        st.close()

    def phase_C2(l, src_lat, src_ctx, dst_lat, dst_ctx, do_ctx):
        st = ExitStack()
        T = norm_tiles(st)
        T.ss2 = sb(st, "ss2b", [128, 4])
        P.G = sb(st, "Grow", [128, 2, 2, D])
        S.dma("sp", P.G[:, :, :, :].rearrange("p a b d -> p (a b d)"), X.Gd.t, [X.Gd], [P.G], P.G)
        W1 = load_w_bf16(st, "w1", I.mlp_w1[l], 8, 4 * D)
        W2 = load_w_bf16(st, "w2", I.mlp_w2[l], 32, D)
        hT = sb(st, "c2hT", [128, 8, 256], BF16)
        h1 = sb(st, "c2h1", [128, 32, 256], BF16)
        rl = [sb(st, "c2rl%d" % i, [128, 256]) for i in range(2)]
        xk = [sb(st, "c2xk%d" % i, [128, D]) for i in range(2)]
        res = sb(st, "c2res", [128, D])
        pb = [psb(st, "c2p%d" % i, [128, 512]) for i in range(6)]
        tiles = ([(1, 0)] if do_ctx else []) + [(0, i * 256) for i in range(16)]
        k = 0
        for (w, t0) in tiles[:ntlC]:
            sbuf_, dbuf_ = (src_ctx, dst_ctx) if w == 1 else (src_lat, dst_lat)
            for sub in range(2):
                make_hT(T, [(sbuf_.t[t0 + sub * 128:t0 + sub * 128 + 128, :], 0, 128)], [sbuf_], w, 1, hT, sub * 128, keep_x=xk[sub])
            for hc in range(32):
                p_ = pb[2 + (k % 4)]
                rr = rl[k % 2]
                k += 1

                def f(e, p_=p_, hc=hc):
                    for dc in range(8):
                        ins = e.matmul(p_[:, 0:256], W1[:, dc, hc * 128:(hc + 1) * 128], hT[:, dc, :], start=(dc == 0), stop=(dc == 7))
                    return ins
                S.op("pe", [W1, hT], [p_], f)
                S.op("act", [p_], [rr], lambda e, p_=p_, rr=rr: e.activation(out=rr[:, :], in_=p_[:, 0:256], func=AF.Relu))
                eng = "dve" if hc % 2 == 0 else "pool"
                S.op(eng, [rr], [h1], lambda e, rr=rr, hc=hc: e.tensor_tensor(out=h1[:, hc, :], in0=rr[:, :], in1=rr[:, :], op=ALU.mult))
            for sub in range(2):
                pm2 = [pb[0], pb[1]]
                for n in range(2):
                    def f(e, n=n, sub=sub):
                        for hc in range(32):
                            ins = e.matmul(pm2[n][:, :], h1[:, hc, sub * 128:(sub + 1) * 128], W2[:, hc, n * 512:(n + 1) * 512], start=(hc == 0), stop=(hc == 31))
                        return ins
                    S.op("pe", [h1, W2], [pm2[n]], f)
                r0 = t0 + sub * 128
                rms_residual(T, pm2, xk[sub], w, 1, dbuf_.t[r0:r0 + 128, :], dbuf_, res)
        S.barrier()
        st.close()

    def run():
        src = (I.x_lat, I.x_ctx)
        for l in range(nlayers):
            last = l == 1
            phase_mod(l)
            if stop == "mod%d" % l:
                return
            phase_A(l, src[0], src[1], "rw")
            phase_A(l, src[0], src[1], "gla")
            if stop == "A%d" % l:
                return
            phase_B1(l, 0)
            phase_B1(l, 1)
            if stop == "B1%d" % l:
                return
            phase_B2(l, 0)
            phase_B2(l, 1)
            if stop == "B2%d" % l:
                return
            phase_C1(l, src[0], src[1], X.xm_lat, X.xm_ctx, not last)
            if stop == "C1%d" % l:
                return
            dst = (out_t, None) if last else (X.x1_lat, X.x1_ctx)
            phase_C2(l, X.xm_lat, X.xm_ctx, dst[0], dst[1], not last)
            src = dst
    run()
    S.barrier()
    if dbg:
        pass
    es.close()
    return nc


def prep_inputs(inp, b):
    f = np.ascontiguousarray

    def fm(v, nch):
        sh = v.shape[:-1]
        return f(np.moveaxis(v.reshape(sh + (nch, 128)), -1, -2))
    m = {}
    m["x_lat"] = f(inp["x"][b])
    m["x_ctx"] = f(inp["ctx"][b])
    cf = np.stack([inp["c"][b], inp["c_ctx"]], axis=-1)
    m["cfm"] = f(cf.reshape(8, 128, 2).transpose(1, 0, 2))
    m["ada_w"] = inp["ada_w"]
    m["adab_fm"] = fm(inp["ada_b"], 48)
    m["ada_b"] = inp["ada_b"]
    m["w_in"] = inp["w_in"]
    npre = np.stack([inp["norm_mix_pre"], inp["norm_ffn_pre"]], axis=-1)
    m["npre_fm"] = f(npre.reshape(2, 8, 128, 2).transpose(0, 2, 1, 3))
    m["npost"] = f(np.stack([inp["norm_mix_post"], inp["norm_ffn_post"]], axis=1))
    m["rw_mu_fm"] = fm(inp["rw_mu"], 15)
    m["rw_w0_fm"] = fm(inp["rw_w0"], 4)
    m["rw_a0_fm"] = fm(inp["rw_a0"], 4)
    m["rw_w_up"] = f(inp["rw_w_up"].reshape(2, 128, 512))
    m["rw_a_up"] = f(inp["rw_a_up"].reshape(2, 128, 512))
    m["rw_g_up"] = inp["rw_g_up"]
    kk = np.stack([inp["rw_k_k"], inp["rw_k_a"], inp["rw_r_k"].reshape(2, 512)], axis=-1)
    m["kk_fm"] = f(kk.reshape(2, 4, 128, 3).transpose(0, 2, 1, 3))
    m["rw_gn"] = f(np.stack([inp["rw_gn_w"], inp["rw_gn_b"]], axis=1))
    m["vres_down"] = f(inp["rw_vres_down"][0].reshape(4, 128, 32).transpose(1, 0, 2))
    m["vres_up"] = f(inp["rw_vres_up"][0])
    m["vres_bias"] = f(inp["rw_vres_bias"])
    for k in ("rw_out", "gla_out", "merge_out", "mlp_w1", "mlp_w2", "gla_alpha_up", "gla_norm_w"):
        m[k] = inp[k]
    gc = inp["gla_conv"]
    m["gla_conv_fm"] = f(gc.reshape(2, 3, 8, 128).transpose(0, 3, 2, 1))
    m["gla_abias_fm"] = fm(inp["gla_alpha_bias"], 2)
    return {k: np.ascontiguousarray(v, dtype=np.float32) for k, v in m.items()}


def kernel(**inputs):
    inp = {k: np.asarray(v) for k, v in inputs.items()}
    nc = build()
    in_maps = [prep_inputs(inp, c % 4) for c in range(8)]
    res = run_bass_kernel_spmd(nc, in_maps, core_ids=list(range(8)))
    out = np.stack([np.asarray(res.results[b]["out"]) for b in range(4)], axis=0)
    return out.astype(np.float32)
```

```python
import numpy as np
from contextlib import ExitStack
import concourse.bass as bass
import concourse.mybir as mybir
from concourse.bass_utils import run_bass_kernel_spmd

F32 = mybir.dt.float32
F32R = mybir.dt.float32r
BF16 = mybir.dt.bfloat16
AF = mybir.ActivationFunctionType
ALU = mybir.AluOpType
AX = mybir.AxisListType

D = 1024
TC = 256
TL = 4096
OFFC = 1
OFFL = 259
TW = 4356
EPS = 1e-6
C0_RW = float(np.exp(-0.5))
C0_GLA = 1.0 / 16.0


class Buf:
    def __init__(self, t, name):
        self.t = t
        self.name = name
        self.w = {}
        self.r = {}
        self.sem = None
        self.semval = 0
        self.sidx = None
        self.sphase = -1

    def __getitem__(self, k):
        return self.t[k]


class View(Buf):
    def __init__(self, base, ap):
        self.base = base
        self.t = ap
        self.name = base.name

    w = property(lambda s: s.base.w, lambda s, v: setattr(s.base, "w", v))
    r = property(lambda s: s.base.r, lambda s, v: setattr(s.base, "r", v))
    sidx = property(lambda s: s.base.sidx, lambda s, v: setattr(s.base, "sidx", v))
    sphase = property(lambda s: s.base.sphase, lambda s, v: setattr(s.base, "sphase", v))


class Sched:
    def __init__(self, nc, es):
        self.nc = nc
        self.es = es
        self.engs = {"pe": nc.tensor, "act": nc.scalar, "dve": nc.vector, "pool": nc.gpsimd, "sp": nc.sync}
        self.sem = {}
        self.cnt = {}
        for e in ("pe", "act", "dve", "pool"):
            self.sem[e] = es.enter_context(nc.semaphore("s_" + e))
            self.cnt[e] = 0
        self.waited = {}
        self.latest = {}
        self.nsem = 0
        self.dpool = []
        self.dfree = {"hw": [], "sw": []}
        self.phase = 0

    def _wait(self, e, toks):
        eng = self.engs[e]
        for name, (sem, val) in toks.items():
            key = (e, name)
            if self.waited.get(key, 0) < val:
                eng.wait_ge(sem, val)
                self.waited[key] = val

    def _deps(self, e, R, W):
        toks = {}

        def add(d):
            for k, (s, v) in d.items():
                if k not in toks or toks[k][1] < v:
                    toks[k] = (s, v)
        for b in R:
            add(b.w)
        for b in W:
            add(b.w)
            add(b.r)
        self._wait(e, toks)

    def op(self, e, R, W, fn):
        self._deps(e, R, W)
        ins = fn(self.engs[e])
        self.cnt[e] += 1
        ins.then_inc(self.sem[e], 1)
        name = "s_" + e
        tok = (self.sem[e], self.cnt[e])
        self.latest[name] = tok
        for b in R:
            b.r[name] = tok
        for b in W:
            b.w = {name: tok}
            b.r = {}
        return ins

    def dma(self, q, out, in_, R, W, owner, **kw):
        self._deps(q, R, W)
        if owner.sidx is None or owner.sphase != self.phase:
            kind = "sw" if q == "pool" else "hw"
            if self.dfree[kind]:
                owner.sidx = self.dfree[kind].pop()
            else:
                self.nsem += 1
                nm = "dq%d" % self.nsem
                self.dpool.append([self.es.enter_context(self.nc.semaphore(nm)), 0, nm, kind])
                owner.sidx = len(self.dpool) - 1
            owner.sphase = self.phase
        ent = self.dpool[owner.sidx]
        ent[1] += 16
        ins = self.engs[q].dma_start(out=out, in_=in_, **kw)
        ins.then_inc(ent[0], 16)
        name = ent[2]
        tok = (ent[0], ent[1])
        self.latest[name] = tok
        for b in R:
            b.r[name] = tok
        for b in W:
            b.w = {name: tok}
            b.r = {}
        return ins

    def barrier(self):
        for e in self.engs:
            self._wait(e, dict(self.latest))
        self.phase += 1
        self.dfree = {k: [i for i, ent in enumerate(self.dpool) if ent[3] == k] for k in ("hw", "sw")}


class Ctx:
    pass


class StopPhase(Exception):
    pass


def build(dbg=None):
    nc = bass.Bass("TRN2", target_bir_lowering=False)
    es = ExitStack()
    S = Sched(nc, es)
    skind = "ExternalOutput" if dbg else "Internal"
    stop = dbg.get("stop") if dbg else None
    nlayers = dbg.get("nlayers", 2) if dbg else 2
    ntl = dbg.get("ntiles", 17) if dbg else 17
    ntlA = dbg.get("ntilesA", 17) if dbg else 17
    stage_lim = dbg.get("stage", 99) if dbg else 99
    ntlC = dbg.get("ntilesC", 99) if dbg else 99

    def stage(n):
        if n >= stage_lim:
            raise StopPhase()

    def din(name, shape, dt=F32):
        return nc.dram_tensor(name, list(shape), dt, kind="ExternalInput").ap()

    def dscr(name, shape, dt=F32):
        return Buf(nc.dram_tensor(name, list(shape), dt, kind=skind).ap(), name)

    I = Ctx()
    I.x_lat = Buf(din("x_lat", [TL, D]), "x_lat")
    I.x_ctx = Buf(din("x_ctx", [TC, D]), "x_ctx")
    I.cfm = din("cfm", [128, 8, 2])
    I.ada_w = din("ada_w", [2, D, 6 * D])
    I.adab_fm = din("adab_fm", [2, 128, 48])
    I.ada_b = din("ada_b", [2, 6 * D])
    I.w_in = din("w_in", [2, D, 5536])
    I.npre_fm = din("npre_fm", [2, 128, 8, 2])
    I.npost = din("npost", [2, 2, D])
    I.rw_mu_fm = din("rw_mu_fm", [2, 128, 15])
    I.rw_w0_fm = din("rw_w0_fm", [2, 2, 128, 4])
    I.rw_a0_fm = din("rw_a0_fm", [2, 2, 128, 4])
    I.rw_w_up = din("rw_w_up", [2, 128, 512])
    I.rw_a_up = din("rw_a_up", [2, 128, 512])
    I.rw_g_up = din("rw_g_up", [2, 128, 512])
    I.kk_fm = din("kk_fm", [2, 128, 4, 3])
    I.rw_gn = din("rw_gn", [2, 2, 512])
    I.vres_down = din("vres_down", [128, 4, 32])
    I.vres_up = din("vres_up", [32, 512])
    I.vres_bias = din("vres_bias", [1, 512])
    I.rw_out = din("rw_out", [2, 512, D])
    I.gla_out = din("gla_out", [2, 512, D])
    I.merge_out = din("merge_out", [2, D, D])
    I.mlp_w1 = din("mlp_w1", [2, D, 4 * D])
    I.mlp_w2 = din("mlp_w2", [2, 4 * D, D])
    I.gla_conv_fm = din("gla_conv_fm", [2, 128, 8, 3])
    I.gla_alpha_up = din("gla_alpha_up", [2, 2, 16, 256])
    I.gla_abias_fm = din("gla_abias_fm", [2, 2, 128, 2])
    I.gla_norm_w = din("gla_norm_w", [2, 128])
    out_t = Buf(nc.dram_tensor("out", [TL, D], F32, kind="ExternalOutput").ap(), "out")

    X = Ctx()
    X.rawrw = dscr("rawrw", [15 * 128, TW])
    X.rawgl = dscr("rawgl", [10 * 128, TW])
    X.gatetm = dscr("gatetm", [TC + TL, 512])
    X.yrwf = dscr("yrwf", [2, TC + TL, 256])
    X.yrwb = dscr("yrwb", [2, TC + TL, 4])
    X.yrw = dscr("yrw", [TC + TL, 512])
    X.yglf = dscr("yglf", [2, TC + TL, 256])
    X.ygl = dscr("ygl", [TC + TL, 512])
    X.vfirst = dscr("vfirst", [TC + TL, 512])
    X.xm_lat = dscr("xm_lat", [TL, D])
    X.xm_ctx = dscr("xm_ctx", [TC, D])
    X.x1_lat = dscr("x1_lat", [TL, D])
    X.x1_ctx = dscr("x1_ctx", [TC, D])
    X.Gd = dscr("Gd", [128, 4 * D])

    uniq = [0]

    def sb(stack, name, shape, dt=F32):
        uniq[0] += 1
        name = "%s_%d" % (name, uniq[0])
        return Buf(stack.enter_context(nc.sbuf_tensor(name, list(shape), dt)), name)

    def psb(stack, name, shape, dt=F32):
        uniq[0] += 1
        name = "%s_%d" % (name, uniq[0])
        return Buf(stack.enter_context(nc.psum_tensor(name, list(shape), dt)), name)

    P = Ctx()
    P.ident_f = sb(es, "ident_f", [128, 128])
    P.ident_b = sb(es, "ident_b", [128, 128], BF16)
    P.ones_f = sb(es, "ones_f", [128, 128])
    P.zeros_f = sb(es, "zeros_f", [128, 512])
    P.bones = sb(es, "bones", [128, 128])
    P.m_ge_f = sb(es, "m_ge_f", [128, 4, 64])
    P.m_gt_f = sb(es, "m_gt_f", [128, 4, 64])
    P.m_ge_b = sb(es, "m_ge_b", [128, 4, 64])
    P.m_gt_b = sb(es, "m_gt_b", [128, 4, 64])
    P.rmask = sb(es, "rmask", [128, 256])
    P.cfm = sb(es, "cfm_s", [128, 8, 2])
    P.scs = sb(es, "scs", [128, 8, 2])
    P.modF = sb(es, "modF", [128, 48, 2])
    P.SB_ = sb(es, "SBm", [128, 8, 2, 2, 2])
    P.eps = sb(es, "epsc", [128, 4])

    def consts():
        S.op("pool", [], [P.ones_f], lambda e: e.memset(P.ones_f[:, :], 1.0))
        S.op("pool", [], [P.zeros_f], lambda e: e.memset(P.zeros_f[:, :], 0.0))
        S.op("pool", [P.ones_f], [P.ident_f], lambda e: e.affine_select(
            out=P.ident_f[:, :], in_=P.ones_f[:, :], pattern=[[-1, 128]], compare_op=ALU.is_equal,
            fill=0.0, base=0, channel_multiplier=1))
        S.op("pool", [P.ident_f], [P.ident_b], lambda e: e.tensor_copy(out=P.ident_b[:, :], in_=P.ident_f[:, :]))
        S.op("pool", [], [P.bones], lambda e: e.memset(P.bones[:, :], 0.0))
        S.op("pool", [], [P.bones], lambda e: e.memset(P.bones[0:64, 0:64], 1.0))
        S.op("pool", [], [P.bones], lambda e: e.memset(P.bones[64:128, 64:128], 1.0))
        ones3 = P.ones_f[0:64, 0:64].unsqueeze(1).to_broadcast([64, 4, 64])
        for m, cm, st, op in ((P.m_ge_f, -1, 1, ALU.is_ge), (P.m_gt_f, -1, 1, ALU.is_gt),
                              (P.m_ge_b, 1, -1, ALU.is_ge), (P.m_gt_b, 1, -1, ALU.is_gt)):
            S.op("pool", [P.ones_f], [m], lambda e, m=m, cm=cm, st=st, op=op: e.affine_select(
                out=m[0:64, :, :], in_=ones3, pattern=[[0, 4], [st, 64]], compare_op=op,
                fill=0.0, base=0, channel_multiplier=cm))
            S.op("dve", [m], [m], lambda e, m=m: e.tensor_copy(out=m[64:128, :, :], in_=m[0:64, :, :]))
        S.op("pool", [], [P.rmask], lambda e: e.memset(P.rmask[:, :], 1.0))
        for c in range(4):
            S.op("pool", [], [P.rmask], lambda e, c=c: e.memset(P.rmask[:, c * 64:c * 64 + 1], 0.0))
        for i, v in enumerate((EPS, 1e-12, 64e-5, 1.0)):
            S.op("pool", [], [P.eps], lambda e, i=i, v=v: e.memset(P.eps[:, i:i + 1], v))
        S.dma("sp", P.cfm[:, :, :], I.cfm, [], [P.cfm], P.cfm)
        S.op("act", [P.cfm], [P.scs], lambda e: e.activation(out=P.scs[:, :, :], in_=P.cfm[:, :, :], func=AF.Silu))

    consts()

    def phase_mod(l):
        st = ExitStack()
        wbuf = [sb(st, "adaw%d" % i, [128, 8, 512]) for i in range(2)]
        bfm = sb(st, "adab", [128, 48])
        npre = sb(st, "npre", [128, 8, 2])
        brow = sb(st, "brow", [128, 2, D])
        prow = sb(st, "prow", [128, 2, D])
        pm = psb(st, "pm", [128, 48, 2])
        P.G = sb(st, "Grow", [128, 2, 2, D])
        pr = [psb(st, "pr%d" % i, [128, 512]) for i in range(2)]
        screp = sb(st, "screp", [128, 8, 2, 128])
        S.op("dve", [P.scs], [screp], lambda e: e.tensor_copy(
            out=screp[:, :, :, :], in_=P.scs[:, :, :].unsqueeze(3).to_broadcast([128, 8, 2, 128])))
        S.dma("sp", bfm[:, :], I.adab_fm[l], [], [bfm], bfm)
        S.dma("sp", npre[:, :, :], I.npre_fm[l], [], [npre], npre)
        for sub in range(2):
            c0 = 2048 if sub == 0 else 5120
            S.dma("sp", brow[:, sub, :], I.ada_b[l, c0:c0 + D].partition_broadcast(128), [], [brow], brow)
            S.dma("sp", prow[:, sub, :], I.npost[l, sub, :].partition_broadcast(128), [], [prow], prow)
        awv = I.ada_w[l].rearrange("(dc p) n -> p dc n", p=128)
        rowblk = {4: (0, 0), 5: (0, 1), 10: (1, 0), 11: (1, 1)}
        k = 0
        for blk in range(12):
            wb = wbuf[blk % 2]
            S.dma("sp", wb[:, :, :], awv[:, :, blk * 512:(blk + 1) * 512], [], [wb], wb)
            for cc in range(4):
                col = blk * 4 + cc

                def f(e, wb=wb, cc=cc, col=col):
                    for dc in range(8):
                        ins = e.matmul(pm[:, col, :], wb[:, dc, cc * 128:(cc + 1) * 128], P.scs[:, dc, :],
                                       start=(dc == 0), stop=(dc == 7))
                    return ins
                S.op("pe", [wb, P.scs], [pm], f)
            if blk in rowblk:
                sub, half = rowblk[blk]
                for w in range(2):
                    pp = pr[k % 2]
                    k += 1

                    def f(e, wb=wb, w=w, pp=pp):
                        for dc in range(8):
                            ins = e.matmul(pp[:, :], screp[:, dc, w, :], wb[:, dc, :], start=(dc == 0), stop=(dc == 7))
                        return ins
                    S.op("pe", [wb, screp], [pp], f)
                    gs = P.G[:, w, sub, half * 512:(half + 1) * 512]
                    S.op("dve", [pp, brow], [P.G], lambda e, pp=pp, gs=gs, sub=sub, half=half: e.tensor_tensor(
                        out=gs, in0=pp[:, :], in1=brow[:, sub, half * 512:(half + 1) * 512], op=ALU.add))
                    S.op("dve", [prow], [P.G], lambda e, gs=gs, sub=sub, half=half: e.tensor_tensor(
                        out=gs, in0=gs, in1=prow[:, sub, half * 512:(half + 1) * 512], op=ALU.mult))
        S.op("dve", [pm, bfm], [P.modF], lambda e: e.tensor_tensor(
            out=P.modF[:, :, :], in0=pm[:, :, :], in1=bfm[:, :].unsqueeze(2).to_broadcast([128, 48, 2]), op=ALU.add))
        for sub in range(2):
            sh0 = 0 if sub == 0 else 24
            sc0 = 8 if sub == 0 else 32
            for w in range(2):
                S.op("dve", [P.modF, npre], [P.SB_], lambda e, sub=sub, w=w, sc0=sc0: e.scalar_tensor_tensor(
                    out=P.SB_[:, :, w, sub, 0], in0=P.modF[:, sc0:sc0 + 8, w], scalar=1.0, in1=npre[:, :, sub],
                    op0=ALU.add, op1=ALU.mult))
                S.op("dve", [P.modF], [P.SB_], lambda e, sub=sub, w=w, sh0=sh0: e.tensor_copy(
                    out=P.SB_[:, :, w, sub, 1], in_=P.modF[:, sh0:sh0 + 8, w]))
        S.dma("sp", X.Gd.t, P.G[:, :, :, :].rearrange("p a b d -> p (a b d)"), [P.G], [X.Gd], P.G)
        S.barrier()
        st.close()

    def row_pieces(buf_lat, buf_ctx, which, order, s0, n, width=None):
        if which == 1:
            return [(buf_ctx.t[s0:s0 + n, :], 0, n)]
        if order == "row":
            return [(buf_lat.t[s0:s0 + n, :], 0, n)]
        v = buf_lat.t.rearrange("(r c) d -> c r d", c=64)
        res = []
        for i in range(n // 64):
            c = s0 // 64 + i
            res.append((v[c], i * 64, 64))
        return res

    def scr_pieces(buf, which, order, s0, n):
        if which == 1:
            return [(buf.t[s0:s0 + n, :], 0, n)]
        if order == "row":
            return [(buf.t[TC + s0:TC + s0 + n, :], 0, n)]
        v = buf.t[TC:TC + TL, :].rearrange("(r c) d -> c r d", c=64)
        return [(v[s0 // 64 + i], i * 64, 64) for i in range(n // 64)]

    def make_hT(T, pieces, srcbufs, w, sub, hT, col0, keep_x=None):
        xt = keep_x if keep_x is not None else T.xt[T.k % 2]
        T.k += 1
        for (ap, po, m) in pieces:
            S.dma("sp", xt[po:po + m, :], ap, srcbufs, [xt], xt)
        S.op("act", [xt], [T.junk, T.ss], lambda e: e.activation(
            out=T.junk[:, :], in_=xt[:, :], func=AF.Square, accum_out=T.ss[:, 0:1]))
        S.op("act", [T.ss], [T.ss], lambda e: e.activation(
            out=T.ss[:, 1:2], in_=T.ss[:, 0:1], func=AF.Sqrt, bias=P.eps[:, 0:1], scale=1.0 / D))
        S.op("dve", [T.ss], [T.ss], lambda e: e.reciprocal(out=T.ss[:, 2:3], in_=T.ss[:, 1:2]))
        S.op("dve", [xt, T.ss], [T.xn], lambda e: e.tensor_scalar(
            out=T.xn[:, :], in0=xt[:, :], scalar1=T.ss[:, 2:3], scalar2=None, op0=ALU.mult))

        def f(e):
            for dc in range(8):
                ins = e.transpose(T.pt[:, dc, :], T.xn[:, dc * 128:(dc + 1) * 128], P.ident_b[:, :])
            return ins
        S.op("pe", [T.xn], [T.pt], f)
        for dc in range(8):
            S.op("act", [T.pt], [hT], lambda e, dc=dc: e.activation(
                out=hT[:, dc, col0:col0 + 128], in_=T.pt[:, dc, :], func=AF.Identity,
                scale=P.SB_[:, dc, w, sub, 0:1], bias=P.SB_[:, dc, w, sub, 1:2]))
        return xt

    def norm_tiles(st, need_xt=True):
        T = Ctx()
        T.k = 0
        if need_xt:
            T.xt = [sb(st, "xt%d" % i, [128, D]) for i in range(2)]
        T.junk = sb(st, "junk", [128, D], BF16)
        T.ss = sb(st, "ss", [128, 4])
        T.xn = sb(st, "xn", [128, D], BF16)
        T.pt = psb(st, "ptr", [128, 8, 128], BF16)
        return T

    def load_w_bf16(st, name, src_ap, nchunk, ncols, R=()):
        wt = sb(st, name, [128, nchunk, ncols], BF16)
        v = src_ap.rearrange("(c p) n -> p c n", p=128)
        for c in range(nchunk):
            S.dma("pool", wt[:, c, :], v[:, c, :], [], [wt], wt)
        return wt

    TILES = [(1, 0)] + [(0, i * 256) for i in range(16)]

    def phase_A(l, src_lat, src_ctx, branch):
        st = ExitStack()
        order = ("row" if l % 2 == 0 else "col") if branch == "rw" else ("col" if l % 2 == 0 else "row")
        T = norm_tiles(st)
        if branch == "rw":
            Wt = load_w_bf16(st, "wA", I.w_in[l][:, 0:1920], 8, 1920)
            nch = 15
            raw = X.rawrw
        else:
            Wt = load_w_bf16(st, "wA", I.w_in[l][:, 1920:3488], 8, 1568)
            nch = 10
            raw = X.rawgl
        hT = [sb(st, "hT%d" % i, [128, 8, 256], BF16) for i in range(2)]
        stg = [sb(st, "stg%d" % i, [128, nch, 256]) for i in range(2)]
        pp = [psb(st, "ppA%d" % i, [128, 256]) for i in range(3)]
        if branch == "gla":
            gst = [sb(st, "gst%d" % i, [128, 512]) for i in range(2)]
            pg = [psb(st, "pgA%d" % i, [128, 512]) for i in range(2)]
        rawv = raw.t.rearrange("(c p) t -> p c t", p=128)
        for col in (0, 257, 258, 4355):
            S.dma("sp", rawv[:, :, col:col + 1], P.zeros_f[:, 0:nch].unsqueeze(2), [P.zeros_f], [raw], P.zeros_f,
                  allow_slow_non_contiguous=True)
        k = 0
        tilesA = TILES[:ntlA]

        def prepA(ti):
            w, s0 = tilesA[ti]
            for sub in range(2):
                pcs = row_pieces(src_lat, src_ctx, w, order, s0 + sub * 128, 128)
                make_hT(T, pcs, [src_lat, src_ctx], w, 0, hT[ti % 2], sub * 128)
        prepA(0)
        for ti, (w, s0) in enumerate(tilesA):
            h = hT[ti % 2]
            sg = stg[ti % 2]
            for cc in range(nch):
                if cc == nch // 2 and ti + 1 < len(tilesA):
                    prepA(ti + 1)
                if branch == "rw" or cc < 8:
                    cols = (cc * 128, 128)
                elif cc == 8:
                    cols = (1024, 16)
                else:
                    cols = (1040, 16)
                p_ = pp[k % 3]
                k += 1
                M = cols[1]

                def f(e, p_=p_, cols=cols, h=h, M=M):
                    for dc in range(8):
                        ins = e.matmul(p_[0:M, :], Wt[:, dc, cols[0]:cols[0] + M], h[:, dc, :], start=(dc == 0), stop=(dc == 7))
                    return ins
                S.op("pe", [Wt, h], [p_], f)
                eng = "act" if cc % 2 == 0 else "dve"
                if eng == "act":
                    S.op("act", [p_], [sg], lambda e, p_=p_, cc=cc, M=M: e.activation(out=sg[0:M, cc, :], in_=p_[0:M, :], func=AF.Copy))
                else:
                    S.op("dve", [p_], [sg], lambda e, p_=p_, cc=cc, M=M: e.tensor_copy(out=sg[0:M, cc, :], in_=p_[0:M, :]))
            off = (OFFC if w == 1 else OFFL) + s0
            if branch == "rw":
                S.dma("pool", rawv[:, :, off:off + 256], sg[:, :, :], [sg], [raw], sg)
            else:
                S.dma("pool", rawv[:, 0:8, off:off + 256], sg[:, 0:8, :], [sg], [raw], sg)
                S.dma("pool", rawv[0:16, 8:10, off:off + 256], sg[0:16, 8:10, :], [sg], [raw], sg)
                for sub in range(2):
                    g_ = gst[sub]
                    pgs = pg[sub]

                    def f(e, pgs=pgs, sub=sub, h=h):
                        for dc in range(8):
                            ins = e.matmul(pgs[:, :], h[:, dc, sub * 128:(sub + 1) * 128], Wt[:, dc, 1056:1568], start=(dc == 0), stop=(dc == 7))
                        return ins
                    S.op("pe", [Wt, h], [pgs], f)
                    S.op("act", [pgs], [g_], lambda e, pgs=pgs, g_=g_: e.activation(out=g_[:, :], in_=pgs[:, :], func=AF.Copy))
                    r0 = (0 if w == 1 else TC) + s0 + sub * 128
                    S.dma("pool", X.gatetm.t[r0:r0 + 128, :], g_[:, :], [g_], [X.gatetm], g_)
        S.barrier()
        st.close()

    def supertile_order(d):
        if d == 0:
            return [(1, 0)] + [(0, i * 256) for i in range(ntl - 1)]
        return [(1, 0)] + [(0, i * 256) for i in reversed(range(ntl - 1))]

    def hp(p):
        return slice(64 * p, 64 * p + 64)

    def r(ap):
        return ap.bitcast(F32R)

    class PsumPool2:
        def __init__(self, st, n):
            self.b = [psb(st, "pq%d" % i, [128, 1024]) for i in range(n)]
            self.k = 0

        def get(self):
            b = self.b[self.k % len(self.b)]
            self.k += 1
            return b

    def pcol(j, p, slot, wd=64):
        c0 = p * 512 + slot * 256 + j * wd
        return slice(c0, c0 + wd)

    def pview(ps, p, slot, nj=4, wd=64):
        c0 = p * 512 + slot * 256
        return ps[0:64, c0:c0 + nj * wd].rearrange("s (j v) -> s j v", v=wd)

    def evac(ps, slot, dstbuf, dst_fn, nj=4, wd=64, mask=None, engs=("act", "dve"), extraR=()):
        for p in range(2):
            src = pview(ps, p, slot, nj, wd)
            dst = dst_fn(p)
            if mask is not None:
                S.op("dve", [ps, mask] + list(extraR), [dstbuf], lambda e, src=src, dst=dst, p=p: e.tensor_tensor(
                    out=dst, in0=src, in1=mask[hp(p), 0:nj, :], op=ALU.mult))
            elif engs[p] == "act":
                S.op("act", [ps] + list(extraR), [dstbuf], lambda e, src=src, dst=dst: e.activation(out=dst, in_=src, func=AF.Copy))
            else:
                S.op("dve", [ps] + list(extraR), [dstbuf], lambda e, src=src, dst=dst: e.tensor_copy(out=dst, in_=src))

    def cum_block(T, SG, nj, d, c0):
        for j in range(nj):
            S.op("dve", [SG, P.rmask], [T.CS], lambda e, j=j: e.tensor_tensor_scan(
                out=T.CS[:, j, :], data0=P.rmask[:, :], data1=SG[:, j, :], initial=0.0, op0=ALU.mult, op1=ALU.add))
        csv = T.CS[:, 0:nj, :].rearrange("p j (c t) -> p j c t", t=64)
        S.op("dve", [T.CS], [T.TOT], lambda e: e.tensor_copy(out=T.TOT[:, 0:nj, :], in_=csv[:, :, :, 63]))
        if d == 1:
            S.op("dve", [T.CS, T.TOT], [T.CS], lambda e: e.tensor_tensor(
                out=csv, in0=T.TOT[:, 0:nj, :].unsqueeze(3).to_broadcast([128, nj, 4, 64]), in1=csv, op=ALU.subtract))
            S.op("dve", [SG], [T.CS], lambda e: e.tensor_tensor(
                out=T.CS[:, 0:nj, :], in0=T.CS[:, 0:nj, :], in1=SG[:, 0:nj, :], op=ALU.add))
        S.op("act", [T.CS], [T.E1], lambda e: e.activation(out=T.E1[:, 0:nj, :], in_=T.CS[:, 0:nj, :], func=AF.Exp, scale=-c0))
        S.op("act", [T.CS], [T.E2], lambda e: e.activation(out=T.E2[:, 0:nj, :], in_=T.CS[:, 0:nj, :], func=AF.Exp, scale=c0))
        S.op("act", [T.TOT], [T.Wc], lambda e: e.activation(out=T.Wc[:, 0:nj, :], in_=T.TOT[:, 0:nj, :], func=AF.Exp, scale=-c0))

    def run_phase(body, *a):
        st = ExitStack()
        try:
            body(*a, st)
        except StopPhase:
            pass
        S.barrier()
        st.close()

    def phase_B1(l, d):
        run_phase(phase_B1_body, l, d)

    def lockstep(*gens):
        gens = [g for g in gens if g is not None]
        while gens:
            nxt = []
            for g in gens:
                try:
                    next(g)
                    nxt.append(g)
                except StopIteration:
                    pass
            gens = nxt

    def chain(*gens):
        for g in gens:
            if g is not None:
                yield from g

    def phase_B1_body(l, d, st):
        order = "row" if l % 2 == 0 else "col"
        raw = sb(st, "raw0", [128, 15, 258])
        T1 = sb(st, "T1", [128, 15, 256])
        Fm = T1
        CS = sb(st, "CS", [128, 4, 256])
        TOT = sb(st, "TOT", [128, 4, 4])
        E1 = sb(st, "E1", [128, 4, 256])
        E2 = sb(st, "E2", [128, 4, 256])
        SG = sb(st, "SG", [128, 4, 256])
        Aa = sb(st, "Aa", [128, 4, 256])
        kraw = sb(st, "kraw", [128, 4, 256])
        kkt = sb(st, "kkt", [128, 4, 256])
        tmp = sb(st, "tmp", [128, 4, 256])
        tmp2 = sb(st, "tmp2", [128, 4, 256])
        th = sb(st, "th", [128, 256])
        kt = kraw
        DS = []
        for i in range(2):
            D_ = Ctx()
            D_.qh = sb(st, "qh%d" % i, [128, 4, 256])
            D_.kh = sb(st, "kh%d" % i, [128, 4, 256])
            D_.ah = sb(st, "ah%d" % i, [128, 4, 256])
            D_.bh = sb(st, "bh%d" % i, [128, 4, 256])
            D_.prod = sb(st, "prod%d" % i, [128, 4, 256])
            D_.sgd = sb(st, "sgd%d" % i, [128, 256])
            D_.Vtm = sb(st, "Vtm%d" % i, [128, 4, 4, 64])
            D_.Gtm = sb(st, "Gtm%d" % i, [128, 4, 4, 64])
            D_.bon = sb(st, "bon%d" % i, [128, 4, 4])
            D_.Ysb = sb(st, "Ysb%d" % i, [128, 4, 4, 64])
            D_.T = Ctx()
            D_.T.CS, D_.T.TOT, D_.T.E1, D_.T.E2 = CS, TOT, E1, E2
            D_.T.Wc = sb(st, "Wc%d" % i, [128, 4, 4])
            DS.append(D_)
        KhT = [sb(st, "KhT%d" % i, [128, 4, 64]) for i in range(4)]
        AhT = [sb(st, "AhT%d" % i, [128, 4, 64]) for i in range(4)]
        Am = {nm: [sb(st, "%s%d" % (nm, c), [128, 4, 64]) for c in range(4)] for nm in ("AqkT", "AqaT", "AbkT", "TT")}
        LN = []
        for i in range(2):
            L_ = Ctx()
            L_.Pq = [sb(st, "Pq%d_%d" % (i, k), [128, 4, 2, 64]) for k in range(2)]
            L_.Rr = [sb(st, "Rr%d_%d" % (i, k), [128, 4, 64]) for k in range(2)]
            L_.ps = psb(st, "pre%d" % i, [128, 1024])
            LN.append(L_)
        Zs = sb(st, "Zs", [128, 4, 64])
        Us = sb(st, "Us", [128, 4, 64])
        Hs = [sb(st, "Hs%d" % i, [128, 4, 64]) for i in range(2)]
        pseq = psb(st, "pseq", [128, 1024])
        pw = psb(st, "psw", [128, 1024])
        mu = sb(st, "mu", [128, 15])
        w0 = sb(st, "w0", [128, 4])
        a0 = sb(st, "a0", [128, 4])
        wup = sb(st, "wup", [128, 512])
        aup = sb(st, "aup", [128, 512])
        gup = sb(st, "gup", [128, 512])
        kkp = sb(st, "kkp", [128, 4, 3])
        omk = sb(st, "omk", [128, 4])
        gnw = sb(st, "gnw", [128, 2, 4, 64])
        S.dma("sp", mu[:, :], I.rw_mu_fm[l], [], [mu], mu)
        S.dma("sp", w0[:, :], I.rw_w0_fm[l, d], [], [w0], w0)
        S.dma("sp", a0[:, :], I.rw_a0_fm[l, d], [], [a0], a0)
        S.dma("sp", wup[:, :], I.rw_w_up[l], [], [wup], wup)
        S.dma("sp", aup[:, :], I.rw_a_up[l], [], [aup], aup)
        S.dma("sp", gup[:, :], I.rw_g_up[l], [], [gup], gup)
        S.dma("sp", kkp[:, :, :], I.kk_fm[l], [], [kkp], kkp)
        for i in range(2):
            gv = I.rw_gn[l, i, :].rearrange("(j p v) -> p j v", p=2, v=64)
            for p in range(2):
                S.dma("sp", gnw[hp(p), i, :, :], gv[p].partition_broadcast(64), [], [gnw], gnw)
        gupr = sb(st, "gupr", [128, 512])
        S.op("dve", [gup], [gupr], lambda e: e.tensor_copy(out=r(gupr[:, :]), in_=gup[:, :]))
        S.op("dve", [kkp], [omk], lambda e: e.tensor_scalar(out=omk[:, :], in0=kkp[:, :, 1], scalar1=-1.0, scalar2=1.0, op0=ALU.mult, op1=ALU.add))
        identr = sb(st, "identr", [128, 128])
        S.op("dve", [P.ident_f], [identr], lambda e: e.tensor_copy(out=r(identr[:, :]), in_=P.ident_f[:, :]))
        onesr = sb(st, "onesr", [128, 2])
        S.op("dve", [P.ones_f], [onesr], lambda e: e.tensor_copy(out=r(onesr[:, :]), in_=P.ones_f[:, 0:2]))
        S.op("dve", [P.zeros_f], [Hs[0]], lambda e: e.tensor_copy(out=r(Hs[0][:, :, :]), in_=P.zeros_f[:, 0:256].rearrange("p (j v) -> p j v", v=64)))
        if l == 1:
            vdn = sb(st, "vdn", [128, 4, 32])
            vup = sb(st, "vup", [32, 512])
            vbs = sb(st, "vbs", [128, 4, 64])
            vf = sb(st, "vf", [128, 4, 4, 64])
            zT = sb(st, "zT", [32, 256])
            gsb = sb(st, "gsb", [128, 4, 64])
            S.dma("sp", vdn[:, :, :], I.vres_down, [], [vdn], vdn)
            S.dma("sp", vup[:, :], I.vres_up, [], [vup], vup)
            bvv = I.vres_bias[0, :].rearrange("(j p v) -> p j v", p=2, v=64)
            for p in range(2):
                S.dma("sp", vbs[hp(p), :, :], bvv[p].partition_broadcast(64), [], [vbs], vbs)
        if d == 1:
            yfy = View(tmp2, tmp2[:, :, :].rearrange("p c (j v) -> p c j v", v=64))
            yfb = sb(st, "yfb", [128, 4, 4])
            stat = sb(st, "stat", [128, 16, 4])
            osc = View(tmp, tmp[:, :, :].rearrange("p a (b v) -> p (a b) v", v=64))
        m_incl = P.m_ge_f if d == 0 else P.m_ge_b
        m_strict = P.m_gt_f if d == 0 else P.m_gt_b
        m_strictN = P.m_gt_b if d == 0 else P.m_gt_f
        rawv = X.rawrw.t.rearrange("(c p) t -> p c t", p=128)
        hstate = [0]

        def canon_rows(buf, w, s0, c):
            if w == 1:
                return buf.t[s0 + c * 64:s0 + (c + 1) * 64, :]
            if order == "row":
                return buf.t[TC + s0 + c * 64:TC + s0 + (c + 1) * 64, :]
            v = buf.t[TC:TC + TL, :].rearrange("(r c) d -> c r d", c=64)
            return v[s0 // 64 + c]

        def heads():
            for p in range(2):
                for j in range(4):
                    yield j, p

        def fm_(t_, j, p, cs):
            return r(t_[hp(p), j, cs])

        def preproc(w, s0, D_):
            T = D_.T
            qh, kh, ah, bh, prod, sgd, Vtm = D_.qh, D_.kh, D_.ah, D_.bh, D_.prod, D_.sgd, D_.Vtm
            vR = prod
            rw_ = raw
            off = (OFFC if w == 1 else OFFL) + s0
            S.dma("sp", rw_[:, :, :], rawv[:, :, off - 1:off + 257], [X.rawrw], [rw_], rw_)
            mid = rw_[:, :, 1:257]
            S.op("pool", [rw_], [T1], lambda e: e.tensor_tensor(out=T1[:, :, :], in0=rw_[:, :, 0:256], in1=rw_[:, :, 2:258], op=ALU.add))
            yield
            S.op("dve", [T1, rw_], [T1], lambda e: e.scalar_tensor_tensor(out=T1[:, :, :], in0=T1[:, :, :], scalar=0.5, in1=mid, op0=ALU.mult, op1=ALU.subtract))
            yield
            S.op("pool", [T1, mu], [T1], lambda e: e.tensor_tensor(out=T1[:, :, :], in0=T1[:, :, :], in1=mu[:, :].unsqueeze(2).to_broadcast([128, 15, 256]), op=ALU.mult))
            yield
            S.op("dve", [T1, rw_], [T1], lambda e: e.tensor_tensor(out=T1[:, :, :], in0=T1[:, :, :], in1=mid, op=ALU.add))
            yield
            Fr, Fk, Fv = Fm[:, 0:4, :], Fm[:, 4:8, :], Fm[:, 8:12, :]
            S.op("pool", [Fm], [vR], lambda e: e.tensor_copy(out=r(vR[:, :, :]), in_=Fv))
            yield
            if l == 1:
                def f(e):
                    for j in range(4):
                        ins = e.matmul(pw[0:32, 0:256], vdn[:, j, :], Fm[:, 8 + j, :], start=(j == 0), stop=(j == 3))
                    return ins
                S.op("pe", [vdn, Fm], [pw], f)
                S.op("act", [pw], [zT], lambda e: e.activation(out=zT[:, :], in_=pw[0:32, 0:256], func=AF.Copy))
                for c in range(4):
                    vv = canon_rows(X.vfirst, w, s0, c).rearrange("t (j p v) -> p t j v", p=2, v=64)
                    for p in range(2):
                        S.dma("sp", vf[hp(p), c, :, :], vv[p], [X.vfirst], [vf], vf)
                yield
            for c in range(4):
                pv = pw

                def f(e, c=c):
                    for p in range(2):
                        for j in range(4):
                            ins = e.matmul(pv[0:64, pcol(j, p, 0)], r(vR[hp(p), j, c * 64:(c + 1) * 64]),
                                           r(identr[hp(p), hp(p)]), start=True, stop=True)
                    return ins
                S.op("pe", [vR, identr], [pv], f)
                if l == 0:
                    evac(pv, 0, Vtm, lambda p, c=c: r(Vtm[hp(p), c, :, :]))
                else:
                    S.op("pe", [zT, vup], [pv], lambda e, c=c: e.matmul(
                        pv[64:128, 0:512], zT[:, c * 64:(c + 1) * 64], vup[:, :], start=True, stop=True))
                    for p in range(2):
                        gsrc = pv[64:128, 0:512].rearrange("t (j p v) -> t p j v", p=2, v=64)[:, p, :, :]
                        S.op("dve", [pv, vbs], [gsb], lambda e, p=p, gsrc=gsrc: e.tensor_tensor(out=gsb[hp(p), :, :], in0=gsrc, in1=vbs[hp(p), :, :], op=ALU.add))
                    S.op("act", [gsb], [gsb], lambda e: e.activation(out=gsb[:, :, :], in_=gsb[:, :, :], func=AF.Sigmoid))
                    for p in range(2):
                        pvv = pview(pv, p, 0)
                        S.op("dve", [vf, pv], [vf], lambda e, p=p, c=c, pvv=pvv: e.tensor_tensor(out=vf[hp(p), c, :, :], in0=vf[hp(p), c, :, :], in1=pvv, op=ALU.subtract))
                    S.op("pool", [vf, gsb], [vf], lambda e, c=c: e.tensor_tensor(out=vf[:, c, :, :], in0=vf[:, c, :, :], in1=gsb[:, :, :], op=ALU.mult))
                    for p in range(2):
                        pvv = pview(pv, p, 0)
                        S.op("dve", [vf, pv], [Vtm], lambda e, p=p, c=c, pvv=pvv: e.tensor_tensor(out=r(Vtm[hp(p), c, :, :]), in0=vf[hp(p), c, :, :], in1=pvv, op=ALU.add))
                yield
            if l == 0 and d == 0:
                for c in range(4):
                    vv = canon_rows(X.vfirst, w, s0, c).rearrange("t (j p v) -> p t j v", p=2, v=64)
                    for p in range(2):
                        S.dma("sp", vv[p], Vtm[hp(p), c, :, :], [Vtm], [X.vfirst], Vtm)
            S.op("dve", [Fm, kkp], [kraw], lambda e: e.tensor_tensor(out=kraw[:, :, :], in0=Fk, in1=kkp[:, :, 0:1].to_broadcast([128, 4, 256]), op=ALU.mult))
            yield
            S.op("pool", [kraw], [tmp], lambda e: e.tensor_tensor(out=tmp[:, :, :], in0=kraw[:, :, :], in1=kraw[:, :, :], op=ALU.mult))
            yield

            def f(e):
                for hh in range(2):
                    ins = e.matmul(pw[:, hh * 512:(hh + 1) * 512], P.bones[:, :], tmp[:, 2 * hh:2 * hh + 2, :].rearrange("p a b -> p (a b)"), start=True, stop=True)
                return ins
            S.op("pe", [tmp], [pw], f)
            S.op("act", [pw], [tmp2], lambda e: e.activation(out=tmp2[:, :, :].rearrange("p a b -> p (a b)"), in_=pw[:, :], func=AF.Sqrt, bias=P.eps[:, 1:2], scale=1.0))
            yield
            S.op("dve", [tmp2], [tmp2], lambda e: e.reciprocal(out=tmp2[:, :, :], in_=tmp2[:, :, :]))
            yield
            S.op("dve", [kraw, tmp2], [kkt], lambda e: e.tensor_tensor(out=kkt[:, :, :], in0=kraw[:, :, :], in1=tmp2[:, :, :], op=ALU.mult))
            yield
            hd = hp(d)
            S.op("act", [Fm], [th], lambda e: e.activation(out=th[hd, :], in_=Fm[hd, 12, :], func=AF.Tanh))

            def f(e):
                for j in range(4):
                    ins = e.matmul(pw[:, j * 256:(j + 1) * 256], wup[hd, j * 128:(j + 1) * 128], th[hd, :], start=True, stop=True)
                return ins
            S.op("pe", [wup, th], [pw], f)
            for j in range(4):
                S.op("act", [pw, w0], [SG], lambda e, j=j: e.activation(out=SG[:, j, :], in_=pw[:, j * 256:(j + 1) * 256], func=AF.Sigmoid, bias=w0[:, j:j + 1]))
            yield

            def f(e):
                for j in range(4):
                    ins = e.matmul(pw[:, j * 256:(j + 1) * 256], aup[hd, j * 128:(j + 1) * 128], Fm[hd, 13, :], start=True, stop=True)
                return ins
            S.op("pe", [aup, Fm], [pw], f)
            for j in range(4):
                S.op("act", [pw, a0], [Aa], lambda e, j=j: e.activation(out=Aa[:, j, :], in_=pw[:, j * 256:(j + 1) * 256], func=AF.Sigmoid, bias=a0[:, j:j + 1]))
            S.op("act", [Fm], [sgd], lambda e: e.activation(out=r(sgd[:, :]), in_=Fm[:, 14, :], func=AF.Sigmoid))
            yield
            cum_block(T, SG, 4, d, C0_RW)
            yield
            S.op("dve", [T.CS, SG], [SG], lambda e: e.tensor_tensor(out=SG[:, :, :], in0=T.CS[:, :, :], in1=SG[:, :, :], op=ALU.subtract))
            S.op("act", [SG], [SG], lambda e: e.activation(out=SG[:, :, :], in_=SG[:, :, :], func=AF.Exp, scale=-C0_RW))
            E3 = SG
            yield
            S.op("dve", [Fm, T.E1], [qh], lambda e: e.tensor_tensor(out=r(qh[:, :, :]), in0=Fr, in1=T.E1[:, :, :], op=ALU.mult))
            yield
            for j in range(4):
                S.op("dve", [Aa, kkp, omk], [tmp], lambda e, j=j: e.tensor_scalar(
                    out=tmp[:, j, :], in0=Aa[:, j, :], scalar1=kkp[:, j, 1:2], scalar2=omk[:, j:j + 1], op0=ALU.mult, op1=ALU.add))
            yield
            S.op("pool", [Fm, tmp], [kt], lambda e: e.tensor_tensor(out=kt[:, :, :], in0=Fk, in1=tmp[:, :, :], op=ALU.mult))
            yield
            S.op("pool", [Aa, kkt], [tmp2], lambda e: e.tensor_tensor(out=tmp2[:, :, :], in0=Aa[:, :, :], in1=kkt[:, :, :], op=ALU.mult))
            yield
            S.op("dve", [tmp2, T.E2], [ah], lambda e: e.scalar_tensor_tensor(out=r(ah[:, :, :]), in0=tmp2[:, :, :], scalar=-1.0, in1=T.E2[:, :, :], op0=ALU.mult, op1=ALU.mult))
            yield
            S.op("dve", [kt, T.E2], [kh], lambda e: e.tensor_tensor(out=r(kh[:, :, :]), in0=kt[:, :, :], in1=T.E2[:, :, :], op=ALU.mult))
            yield
            S.op("dve", [kkt, E3], [bh], lambda e: e.tensor_tensor(out=r(bh[:, :, :]), in0=kkt[:, :, :], in1=E3[:, :, :], op=ALU.mult))
            yield
            S.op("pool", [Fm, kkp], [tmp], lambda e: e.tensor_tensor(out=tmp[:, :, :], in0=Fr, in1=kkp[:, :, 2:3].to_broadcast([128, 4, 256]), op=ALU.mult))
            yield
            S.op("dve", [tmp, kt], [prod], lambda e: e.tensor_tensor(out=r(prod[:, :, :]), in0=tmp[:, :, :], in1=kt[:, :, :], op=ALU.mult))
            yield

        def pre(c, sl, D_, L_):
            qh, kh, ah, bh, prod, sgd, bon, Gtm = D_.qh, D_.kh, D_.ah, D_.bh, D_.prod, D_.sgd, D_.bon, D_.Gtm
            ps = L_.ps
            Pq, Rr = L_.Pq, L_.Rr
            cs = slice(c * 64, (c + 1) * 64)

            def f(e):
                for j, p in heads():
                    ins = e.matmul(ps[0:64, p * 512 + 2 * j:p * 512 + 2 * j + 2], fm_(prod, j, p, cs), r(onesr[hp(p), :]), start=True, stop=True)
                return ins
            S.op("pe", [prod, onesr], [ps], f)
            for p in range(2):
                S.op("act", [ps], [bon], lambda e, p=p: e.activation(
                    out=bon[hp(p), c, :], in_=ps[0:64, p * 512:p * 512 + 8].rearrange("s (j two) -> s j two", two=2)[:, :, 0], func=AF.Copy))
            yield

            def f(e):
                for slot, src in ((0, kh), (1, ah)):
                    for j, p in heads():
                        ins = e.matmul(ps[0:64, pcol(j, p, slot)], fm_(src, j, p, cs), r(identr[hp(p), hp(p)]), start=True, stop=True)
                return ins
            S.op("pe", [kh, ah, identr], [ps], f)
            evac(ps, 0, KhT[sl], lambda p: r(KhT[sl][hp(p), :, :]), engs=("act", "act"))
            evac(ps, 1, AhT[sl], lambda p: r(AhT[sl][hp(p), :, :]), engs=("act", "act"))
            yield

            def f(e):
                for slot, lt, rt in ((0, kh, qh), (1, ah, qh)):
                    for j, p in heads():
                        ins = e.matmul(ps[0:64, pcol(j, p, slot)], fm_(lt, j, p, cs), fm_(rt, j, p, cs), start=True, stop=True)
                return ins
            S.op("pe", [kh, ah, qh], [ps], f)
            evac(ps, 0, Am["AqkT"][sl], lambda p: r(Am["AqkT"][sl][hp(p), :, :]), mask=m_incl)
            evac(ps, 1, Am["AqaT"][sl], lambda p: r(Am["AqaT"][sl][hp(p), :, :]), mask=m_incl)
            yield

            def f(e):
                for j, p in heads():
                    ins = e.matmul(ps[0:64, pcol(j, p, 0)], fm_(kh, j, p, cs), fm_(bh, j, p, cs), start=True, stop=True)
                return ins
            S.op("pe", [kh, bh], [ps], f)
            evac(ps, 0, Am["AbkT"][sl], lambda p: r(Am["AbkT"][sl][hp(p), :, :]), mask=m_strict)
            yield

            def f(e):
                for slot, lt, rt in ((0, bh, ah), (1, ah, bh)):
                    for j, p in heads():
                        ins = e.matmul(ps[0:64, pcol(j, p, slot)], fm_(lt, j, p, cs), fm_(rt, j, p, cs), start=True, stop=True)
                return ins
            S.op("pe", [ah, bh], [ps], f)
            pq = Pq[0]
            evac(ps, 0, pq, lambda p: r(pq[hp(p), :, 0, :]), mask=m_strictN)
            evac(ps, 1, pq, lambda p: r(pq[hp(p), :, 1, :]), mask=m_strict)
            rcur = Rr[0]
            for p in range(2):
                S.op("pool", [pq, identr], [rcur], lambda e, p=p, rcur=rcur: e.tensor_tensor(
                    out=r(rcur[hp(p), :, :]), in0=pq[hp(p), :, 1, :], in1=identr[hp(p), hp(p)].unsqueeze(1).to_broadcast([64, 4, 64]), op=ALU.add))
            yield
            pcur = pq
            for lev in range(5):
                pnx = Pq[(lev + 1) % 2]

                def f(e, pcur=pcur, lev=lev):
                    for j, p in heads():
                        ins = e.matmul(ps[0:64, pcol(j, p, 0)], r(pcur[hp(p), j, 1, :]), r(pcur[hp(p), j, 0, :]), start=True, stop=True)
                    if lev < 4:
                        for j, p in heads():
                            ins = e.matmul(ps[0:64, pcol(j, p, 1)], r(pcur[hp(p), j, 0, :]), r(pcur[hp(p), j, 1, :]), start=True, stop=True)
                    return ins
                S.op("pe", [pcur], [ps], f)
                evac(ps, 0, pnx, lambda p, pnx=pnx: r(pnx[hp(p), :, 0, :]), engs=("act", "act"))
                if lev < 4:
                    evac(ps, 1, pnx, lambda p, pnx=pnx: r(pnx[hp(p), :, 1, :]), engs=("dve", "dve"))
                yield
                rnx = Rr[(lev + 1) % 2] if lev < 4 else Am["TT"][sl]

                def f(e, pnx=pnx, rcur=rcur):
                    for j, p in heads():
                        ins = e.matmul(ps[0:64, pcol(j, p, 0)], r(pnx[hp(p), j, 0, :]), r(rcur[hp(p), j, :]), start=True, stop=True)
                    return ins
                S.op("pe", [pnx, rcur], [ps], f)
                for p in range(2):
                    S.op("dve", [ps, rcur], [rnx], lambda e, p=p, rnx=rnx, rcur=rcur: e.tensor_tensor(
                        out=r(rnx[hp(p), :, :]), in0=pview(ps, p, 0), in1=rcur[hp(p), :, :], op=ALU.add))
                yield
                rcur = rnx
                pcur = pnx
            if d == 1:
                S.op("pe", [sgd, gupr], [ps], lambda e: e.matmul(ps[0:64, 0:512], r(sgd[:, cs]), r(gupr[:, :]), start=True, stop=True))
                for p in range(2):
                    gsrc = ps[0:64, 0:512].rearrange("t (j p v) -> t p j v", p=2, v=64)[:, p, :, :]
                    S.op("act", [ps], [Gtm], lambda e, p=p, gsrc=gsrc: e.activation(out=Gtm[hp(p), c, :, :], in_=gsrc, func=AF.Copy))
                yield

        def seq(c, sl, D_):
            qh, bh, Vtm, Ysb, T = D_.qh, D_.bh, D_.Vtm, D_.Ysb, D_.T
            cs = slice(c * 64, (c + 1) * 64)
            Hc, Hn = Hs[hstate[0]], Hs[1 - hstate[0]]
            hstate[0] = 1 - hstate[0]
            pz = pseq

            def f(e):
                for j, p in heads():
                    e.matmul(pz[0:64, pcol(j, p, 0)], r(Am["AbkT"][sl][hp(p), j, :]), r(Vtm[hp(p), c, j, :]), start=True, stop=False)
                    ins = e.matmul(pz[0:64, pcol(j, p, 0)], fm_(bh, j, p, cs), r(Hc[hp(p), j, :]), start=False, stop=True)
                return ins
            S.op("pe", [Am["AbkT"][sl], Vtm, bh, Hc], [pz], f)
            evac(pz, 0, Zs, lambda p: r(Zs[hp(p), :, :]), engs=("act", "dve"))
            yield

            def f(e):
                for j, p in heads():
                    ins = e.matmul(pz[0:64, pcol(j, p, 1)], r(Am["TT"][sl][hp(p), j, :]), r(Zs[hp(p), j, :]), start=True, stop=True)
                return ins
            S.op("pe", [Am["TT"][sl], Zs], [pz], f)
            evac(pz, 1, Us, lambda p: r(Us[hp(p), :, :]), engs=("act", "dve"))
            yield

            def f(e):
                for j, p in heads():
                    oc = pz[0:64, pcol(j, p, 0)]
                    e.matmul(oc, r(identr[hp(p), hp(p)]), r(Hc[hp(p), j, :]), start=True, stop=False)
                    e.matmul(oc, r(KhT[sl][hp(p), j, :]), r(Vtm[hp(p), c, j, :]), start=False, stop=False)
                    ins = e.matmul(oc, r(AhT[sl][hp(p), j, :]), r(Us[hp(p), j, :]), start=False, stop=True)
                return ins
            S.op("pe", [identr, Hc, KhT[sl], Vtm, AhT[sl], Us], [pz], f)
            for p in range(2):
                S.op("dve", [pz, T.Wc], [Hn], lambda e, p=p: e.tensor_tensor(
                    out=r(Hn[hp(p), :, :]), in0=pview(pz, p, 0),
                    in1=T.Wc[hp(p), :, c:c + 1].to_broadcast([64, 4, 64]), op=ALU.mult))
            yield

            def f(e):
                for j, p in heads():
                    oc = pz[0:64, pcol(j, p, 1)]
                    e.matmul(oc, fm_(qh, j, p, cs), r(Hc[hp(p), j, :]), start=True, stop=False)
                    e.matmul(oc, r(Am["AqkT"][sl][hp(p), j, :]), r(Vtm[hp(p), c, j, :]), start=False, stop=False)
                    ins = e.matmul(oc, r(Am["AqaT"][sl][hp(p), j, :]), r(Us[hp(p), j, :]), start=False, stop=True)
                return ins
            S.op("pe", [qh, Hc, Am["AqkT"][sl], Vtm, Am["AqaT"][sl], Us], [pz], f)
            evac(pz, 1, Ysb, lambda p: Ysb[hp(p), c, :, :], engs=("act", "act"))
            yield

        def output(w, s0, D_):
            Ysb, bon, Gtm, Vtm = D_.Ysb, D_.bon, D_.Gtm, D_.Vtm
            r0 = (0 if w == 1 else TC) + s0
            if d == 0:
                for p in range(2):
                    S.dma("sp", X.yrwf.t[p, r0:r0 + 256, :].rearrange("(c t) n -> t c n", t=64),
                          Ysb[hp(p), :, :, :].rearrange("s c j v -> s c (j v)"), [Ysb], [X.yrwf], Ysb)
                    S.dma("sp", X.yrwb.t[p, r0:r0 + 256, :].rearrange("(c t) n -> t c n", t=64), bon[hp(p), :, :], [bon], [X.yrwb], bon)
                yield
                return
            for p in range(2):
                S.dma("sp", yfy[hp(p), :, :, :].rearrange("s c j v -> s c (j v)"),
                      X.yrwf.t[p, r0:r0 + 256, :].rearrange("(c t) n -> t c n", t=64), [X.yrwf], [yfy], yfy)
                S.dma("sp", yfb[hp(p), :, :], X.yrwb.t[p, r0:r0 + 256, :].rearrange("(c t) n -> t c n", t=64), [X.yrwb], [yfb], yfb)
            y3 = Ysb[:, :, :, :].rearrange("s c j v -> s (c j) v")
            S.op("dve", [yfy, Ysb], [Ysb], lambda e: e.tensor_tensor(
                out=y3, in0=y3, in1=yfy[:, :, :, :].rearrange("s c j v -> s (c j) v"), op=ALU.add))
            yield
            S.op("dve", [Ysb], [stat], lambda e: e.tensor_reduce(out=stat[:, :, 0], in_=y3, axis=AX.X, op=ALU.add))
            S.op("dve", [stat], [stat], lambda e: e.tensor_scalar(out=stat[:, :, 0], in0=stat[:, :, 0], scalar1=1.0 / 64, scalar2=None, op0=ALU.mult))
            yield
            S.op("dve", [Ysb, stat], [Ysb], lambda e: e.tensor_tensor(
                out=y3, in0=y3, in1=stat[:, :, 0:1].to_broadcast([128, 16, 64]), op=ALU.subtract))
            yield
            ysq = osc[:, :, :]
            S.op("pool", [Ysb], [osc], lambda e: e.tensor_tensor(out=ysq, in0=y3, in1=y3, op=ALU.mult))
            yield
            S.op("dve", [osc], [stat], lambda e: e.tensor_reduce(out=stat[:, :, 1], in_=ysq, axis=AX.X, op=ALU.add))
            S.op("act", [stat], [stat], lambda e: e.activation(out=stat[:, :, 2], in_=stat[:, :, 1], func=AF.Sqrt, bias=P.eps[:, 2:3], scale=1.0 / 64))
            S.op("dve", [stat], [stat], lambda e: e.reciprocal(out=stat[:, :, 3], in_=stat[:, :, 2]))
            yield
            S.op("dve", [Ysb, stat], [Ysb], lambda e: e.tensor_tensor(
                out=y3, in0=y3, in1=stat[:, :, 3:4].to_broadcast([128, 16, 64]), op=ALU.mult))
            yield
            for i, op in ((0, ALU.mult), (1, ALU.add)):
                S.op("dve", [Ysb, gnw], [Ysb], lambda e, i=i, op=op: e.tensor_tensor(
                    out=Ysb[:, :, :, :], in0=Ysb[:, :, :, :], in1=gnw[:, i, :, :].unsqueeze(1).to_broadcast([128, 4, 4, 64]), op=op))
                yield
            S.op("dve", [yfb, bon], [stat], lambda e: e.tensor_tensor(
                out=stat[:, :, 0], in0=yfb[:, :, :].rearrange("s c j -> s (c j)"), in1=bon[:, :, :].rearrange("s c j -> s (c j)"), op=ALU.add))
            S.op("pool", [Vtm, stat], [osc], lambda e: e.tensor_tensor(
                out=ysq, in0=Vtm[:, :, :, :].rearrange("s c j v -> s (c j) v"), in1=stat[:, :, 0:1].to_broadcast([128, 16, 64]), op=ALU.mult))
            yield
            S.op("dve", [Ysb, osc], [Ysb], lambda e: e.tensor_tensor(out=y3, in0=y3, in1=ysq, op=ALU.add))
            yield
            S.op("dve", [Ysb, Gtm], [Gtm], lambda e: e.tensor_tensor(out=Gtm[:, :, :, :], in0=Ysb[:, :, :, :], in1=Gtm[:, :, :, :], op=ALU.mult))
            for c in range(4):
                vv = canon_rows(X.yrw, w, s0, c).rearrange("t (j p v) -> p t j v", p=2, v=64)
                for p in range(2):
                    S.dma("sp", vv[p], Gtm[hp(p), c, :, :], [Gtm], [X.yrw], Gtm)
            yield

        tiles = supertile_order(d)
        co = list(range(4)) if d == 0 else [3, 2, 1, 0]
        nt = len(tiles)
        lockstep(preproc(tiles[0][0], tiles[0][1], DS[0]))
        lockstep(pre(co[0], 0, DS[0], LN[0]), pre(co[1], 1, DS[0], LN[1]))
        for ti in range(nt):
            D_ = DS[ti % 2]
            Dn = DS[(ti + 1) % 2]
            side = chain(output(tiles[ti - 1][0], tiles[ti - 1][1], Dn) if ti > 0 else None,
                         preproc(tiles[ti + 1][0], tiles[ti + 1][1], Dn) if ti + 1 < nt else None)
            lockstep(chain(seq(co[0], 0, D_), seq(co[1], 1, D_)),
                     pre(co[2], 2, D_, LN[0]), pre(co[3], 3, D_, LN[1]), side)
            lockstep(chain(seq(co[2], 2, D_), seq(co[3], 3, D_)),
                     pre(co[0], 0, Dn, LN[0]) if ti + 1 < nt else None,
                     pre(co[1], 1, Dn, LN[1]) if ti + 1 < nt else None)
        lockstep(output(tiles[nt - 1][0], tiles[nt - 1][1], DS[(nt - 1) % 2]))

    def phase_B2(l, d):
        run_phase(phase_B2_body, l, d)

    def phase_B2_body(l, d, st):
        order = "col" if l % 2 == 0 else "row"
        T = Ctx()
        raw = sb(st, "graw", [128, 10, 258])
        T1 = sb(st, "gT1", [128, 8, 256])
        T2 = sb(st, "gT2", [128, 8, 256])
        Fm = sb(st, "gFm", [128, 8, 256])
        T.CS = sb(st, "gCS", [128, 2, 256])
        T.TOT = sb(st, "gTOT", [128, 2, 4])
        T.E1 = sb(st, "gE1", [128, 2, 256])
        T.E2 = sb(st, "gE2", [128, 2, 256])
        T.Wc = sb(st, "gWc", [128, 2, 4])
        SG = sb(st, "gSG", [128, 2, 256])
        qh = sb(st, "gqh", [128, 2, 256])
        kh = sb(st, "gkh", [128, 2, 256])
        vR = sb(st, "gvR", [128, 4, 256])
        Vtm = sb(st, "gVtm", [128, 4, 2, 128])
        KhT = [sb(st, "gKhT%d" % i, [128, 2, 64]) for i in range(2)]
        AqkT = [sb(st, "gAqk%d" % i, [128, 2, 64]) for i in range(2)]
        Hs = [sb(st, "gHs%d" % i, [128, 2, 128]) for i in range(2)]
        Ysb = sb(st, "gYsb", [128, 4, 2, 128])
        PS = PsumPool2(st, 3)
        pw = psb(st, "gpw", [128, 512])
        cw = sb(st, "gcw", [128, 8, 3])
        aup = sb(st, "gaup", [16, 256])
        ab = sb(st, "gab", [128, 2])
        nab = sb(st, "gnab", [128, 2])
        S.dma("sp", cw[:, :, :], I.gla_conv_fm[l], [], [cw], cw)
        S.dma("sp", aup[:, :], I.gla_alpha_up[l, d], [], [aup], aup)
        S.dma("sp", ab[:, :], I.gla_abias_fm[l, d], [], [ab], ab)
        S.op("dve", [ab], [nab], lambda e: e.tensor_scalar(out=nab[:, :], in0=ab[:, :], scalar1=-1.0, scalar2=None, op0=ALU.mult))
        identr = sb(st, "gidentr", [128, 128])
        S.op("dve", [P.ident_f], [identr], lambda e: e.tensor_copy(out=r(identr[:, :]), in_=P.ident_f[:, :]))
        S.op("dve", [P.zeros_f], [Hs[0]], lambda e: e.tensor_copy(out=r(Hs[0][:, :, :]), in_=P.zeros_f[:, 0:256].rearrange("p (j v) -> p j v", v=128)))
        if d == 1:
            yf = sb(st, "gyf", [128, 4, 2, 128])
            gt = sb(st, "ggt", [128, 4, 2, 128])
            osq = sb(st, "gosq", [128, 8, 128])
            stat = sb(st, "gstat", [128, 8, 4])
            nw = sb(st, "gnw", [128, 128])
            S.dma("sp", nw[:, :], I.gla_norm_w[l, :].partition_broadcast(128), [], [nw], nw)
        m_incl = P.m_ge_f if d == 0 else P.m_ge_b
        rawv = X.rawgl.t.rearrange("(c p) t -> p c t", p=128)
        hstate = [0]

        def canon_rows(buf, w, s0, c):
            if w == 1:
                return buf.t[s0 + c * 64:s0 + (c + 1) * 64, :]
            if order == "row":
                return buf.t[TC + s0 + c * 64:TC + s0 + (c + 1) * 64, :]
            v = buf.t[TC:TC + TL, :].rearrange("(r c) d -> c r d", c=64)
            return v[s0 // 64 + c]

        def heads():
            for p in range(2):
                for j in range(2):
                    yield j, p
        for ti, (w, s0) in enumerate(supertile_order(d)):
            rw_ = raw
            off = (OFFC if w == 1 else OFFL) + s0
            r0 = (0 if w == 1 else TC) + s0
            S.dma("sp", rw_[:, 0:8, :], rawv[:, 0:8, off - 1:off + 257], [X.rawgl], [rw_], rw_)
            S.dma("sp", rw_[0:16, 8:10, :], rawv[0:16, 8:10, off - 1:off + 257], [X.rawgl], [rw_], rw_)

            def cwb(i):
                return cw[:, :, i:i + 1].to_broadcast([128, 8, 256])
            S.op("pool", [rw_, cw], [T1], lambda e: e.tensor_tensor(out=T1[:, :, :], in0=rw_[:, 0:8, 0:256], in1=cwb(0), op=ALU.mult))
            S.op("dve", [rw_, cw], [T2], lambda e: e.tensor_tensor(out=T2[:, :, :], in0=rw_[:, 0:8, 1:257], in1=cwb(1), op=ALU.mult))
            S.op("pool", [T1, T2], [T1], lambda e: e.tensor_tensor(out=T1[:, :, :], in0=T1[:, :, :], in1=T2[:, :, :], op=ALU.add))
            S.op("dve", [rw_, cw], [T2], lambda e: e.tensor_tensor(out=T2[:, :, :], in0=rw_[:, 0:8, 2:258], in1=cwb(2), op=ALU.mult))
            S.op("pool", [T1, T2], [T1], lambda e: e.tensor_tensor(out=T1[:, :, :], in0=T1[:, :, :], in1=T2[:, :, :], op=ALU.add))
            S.op("act", [T1], [Fm], lambda e: e.activation(out=Fm[:, :, :], in_=T1[:, :, :], func=AF.Silu))

            def f(e):
                for jj in range(2):
                    ins = e.matmul(pw[:, jj * 256:(jj + 1) * 256], aup[:, jj * 128:(jj + 1) * 128], rw_[0:16, 8 + d, 1:257], start=True, stop=True)
                return ins
            S.op("pe", [aup, rw_], [pw], f)
            for jj in range(2):
                S.op("act", [pw, nab], [SG], lambda e, jj=jj: e.activation(out=SG[:, jj, :], in_=pw[:, jj * 256:(jj + 1) * 256], func=AF.Exp, bias=nab[:, jj:jj + 1], scale=-1.0))
            S.op("act", [SG], [SG], lambda e: e.activation(out=SG[:, :, :], in_=SG[:, :, :], func=AF.Ln, bias=P.eps[:, 3:4], scale=1.0))
            cum_block(T, SG, 2, d, C0_GLA)
            S.op("dve", [Fm, T.E1], [qh], lambda e: e.scalar_tensor_tensor(out=r(qh[:, :, :]), in0=Fm[:, 0:2, :], scalar=0.125, in1=T.E1[:, :, :], op0=ALU.mult, op1=ALU.mult))
            S.op("dve", [Fm, T.E2], [kh], lambda e: e.tensor_tensor(out=r(kh[:, :, :]), in0=Fm[:, 2:4, :], in1=T.E2[:, :, :], op=ALU.mult))
            S.op("pool", [Fm], [vR], lambda e: e.tensor_copy(out=r(vR[:, :, :]), in_=Fm[:, 4:8, :]))
            if d == 1:
                for p in range(2):
                    S.dma("sp", yf[hp(p), :, :, :].rearrange("s c j v -> s c (j v)"),
                          X.yglf.t[p, r0:r0 + 256, :].rearrange("(c t) n -> t c n", t=64), [X.yglf], [yf], yf)
                    gv = X.gatetm.t[r0:r0 + 256, :].rearrange("(c t) (j p v) -> p t c j v", t=64, p=2, v=128)
                    for c in range(4):
                        S.dma("sp", gt[hp(p), c, :, :], gv[p][:, c, :, :], [X.gatetm], [gt], gt)

            def pre(c, sl):
                cs = slice(c * 64, (c + 1) * 64)
                pv = PS.get()

                def f(e):
                    for j, p in heads():
                        ins = e.matmul(pv[0:64, p * 512 + j * 128:p * 512 + (j + 1) * 128], r(vR[:, 2 * j + p, cs]), r(identr[:, :]), start=True, stop=True)
                    return ins
                S.op("pe", [vR, identr], [pv], f)
                for p in range(2):
                    S.op("act", [pv], [Vtm], lambda e, p=p: e.activation(
                        out=r(Vtm[hp(p), c, :, :]), in_=pv[0:64, p * 512:p * 512 + 256].rearrange("s (j v) -> s j v", v=128), func=AF.Copy))
                pt_ = PS.get()

                def f(e):
                    for j, p in heads():
                        ins = e.matmul(pt_[0:64, pcol(j, p, 0)], r(kh[hp(p), j, cs]), r(identr[hp(p), hp(p)]), start=True, stop=True)
                    for j, p in heads():
                        ins = e.matmul(pt_[0:64, pcol(j, p, 1)], r(kh[hp(p), j, cs]), r(qh[hp(p), j, cs]), start=True, stop=True)
                    return ins
                S.op("pe", [kh, qh, identr], [pt_], f)
                evac(pt_, 0, KhT[sl], lambda p: r(KhT[sl][hp(p), :, :]), nj=2, engs=("dve", "dve"))
                evac(pt_, 1, AqkT[sl], lambda p: r(AqkT[sl][hp(p), :, :]), nj=2, mask=m_incl)

            def seq(c, sl):
                cs = slice(c * 64, (c + 1) * 64)
                Hc, Hn = Hs[hstate[0]], Hs[1 - hstate[0]]
                ph = PS.get()

                def f(e):
                    for j, p in heads():
                        oc = ph[0:64, p * 512 + j * 128:p * 512 + (j + 1) * 128]
                        e.matmul(oc, r(identr[hp(p), hp(p)]), r(Hc[hp(p), j, :]), start=True, stop=False)
                        ins = e.matmul(oc, r(KhT[sl][hp(p), j, :]), r(Vtm[hp(p), c, j, :]), start=False, stop=True)
                    return ins
                S.op("pe", [identr, Hc, KhT[sl], Vtm], [ph], f)
                for p in range(2):
                    S.op("dve", [ph, T.Wc], [Hn], lambda e, p=p: e.tensor_tensor(
                        out=r(Hn[hp(p), :, :]), in0=ph[0:64, p * 512:p * 512 + 256].rearrange("s (j v) -> s j v", v=128),
                        in1=T.Wc[hp(p), :, c:c + 1].to_broadcast([64, 2, 128]), op=ALU.mult))
                py = PS.get()

                def f(e):
                    for j, p in heads():
                        oc = py[0:64, p * 512 + j * 128:p * 512 + (j + 1) * 128]
                        e.matmul(oc, r(qh[hp(p), j, cs]), r(Hc[hp(p), j, :]), start=True, stop=False)
                        ins = e.matmul(oc, r(AqkT[sl][hp(p), j, :]), r(Vtm[hp(p), c, j, :]), start=False, stop=True)
                    return ins
                S.op("pe", [qh, Hc, AqkT[sl], Vtm], [py], f)
                for p in range(2):
                    S.op("act", [py], [Ysb], lambda e, p=p: e.activation(
                        out=Ysb[hp(p), c, :, :], in_=py[0:64, p * 512:p * 512 + 256].rearrange("s (j v) -> s j v", v=128), func=AF.Copy))
                hstate[0] = 1 - hstate[0]
            co = list(range(4)) if d == 0 else [3, 2, 1, 0]
            pre(co[0], 0)
            pre(co[1], 1)
            seq(co[0], 0)
            pre(co[2], 0)
            seq(co[1], 1)
            pre(co[3], 1)
            seq(co[2], 0)
            seq(co[3], 1)
            if d == 0:
                for p in range(2):
                    S.dma("sp", X.yglf.t[p, r0:r0 + 256, :].rearrange("(c t) n -> t c n", t=64),
                          Ysb[hp(p), :, :, :].rearrange("s c j v -> s c (j v)"), [Ysb], [X.yglf], Ysb)
            else:
                o3 = Ysb[:, :, :, :].rearrange("s c j v -> s (c j) v")
                S.op("dve", [yf, Ysb], [Ysb], lambda e: e.tensor_tensor(out=o3, in0=o3, in1=yf[:, :, :, :].rearrange("s c j v -> s (c j) v"), op=ALU.add))
                S.op("pool", [Ysb], [osq], lambda e: e.tensor_tensor(out=osq[:, :, :], in0=o3, in1=o3, op=ALU.mult))
                S.op("dve", [osq], [stat], lambda e: e.tensor_reduce(out=stat[:, :, 0], in_=osq[:, :, :], axis=AX.X, op=ALU.add))
                S.op("act", [stat], [stat], lambda e: e.activation(out=stat[:, :, 1], in_=stat[:, :, 0], func=AF.Sqrt, bias=P.eps[:, 0:1], scale=1.0 / 128))
                S.op("dve", [stat], [stat], lambda e: e.reciprocal(out=stat[:, :, 2], in_=stat[:, :, 1]))
                S.op("dve", [Ysb, stat], [Ysb], lambda e: e.tensor_tensor(out=o3, in0=o3, in1=stat[:, :, 2:3].to_broadcast([128, 8, 128]), op=ALU.mult))
                S.op("dve", [Ysb, nw], [Ysb], lambda e: e.tensor_tensor(out=o3, in0=o3, in1=nw[:, :].unsqueeze(1).to_broadcast([128, 8, 128]), op=ALU.mult))
                S.op("act", [gt], [yf], lambda e: e.activation(out=yf[:, :, :, :], in_=gt[:, :, :, :], func=AF.Silu))
                S.op("dve", [Ysb, yf], [yf], lambda e: e.tensor_tensor(out=yf[:, :, :, :], in0=Ysb[:, :, :, :], in1=yf[:, :, :, :], op=ALU.mult))
                for c in range(4):
                    vv = canon_rows(X.ygl, w, s0, c).rearrange("t (j p v) -> p t j v", p=2, v=128)
                    for p in range(2):
                        S.dma("sp", vv[p], yf[hp(p), c, :, :], [yf], [X.ygl], yf)

    def rms_residual(T, pm2, xt, w, sub, dst_ap, dstbuf, res):
        for i in range(2):
            S.op("act", [pm2[i]], [T.junk, T.ss2], lambda e, i=i: e.activation(
                out=T.junk[:, 0:512], in_=pm2[i][:, :], func=AF.Square, accum_out=T.ss2[:, i:i + 1]))
        S.op("dve", [T.ss2], [T.ss2], lambda e: e.tensor_tensor(out=T.ss2[:, 2:3], in0=T.ss2[:, 0:1], in1=T.ss2[:, 1:2], op=ALU.add))
        S.op("act", [T.ss2], [T.ss2], lambda e: e.activation(out=T.ss2[:, 3:4], in_=T.ss2[:, 2:3], func=AF.Sqrt, bias=P.eps[:, 0:1], scale=1.0 / D))
        S.op("dve", [T.ss2], [T.ss2], lambda e: e.reciprocal(out=T.ss2[:, 2:3], in_=T.ss2[:, 3:4]))
        for i in range(2):
            hs = slice(i * 512, (i + 1) * 512)
            S.op("dve", [pm2[i], T.ss2, P.G], [res], lambda e, i=i, hs=hs: e.scalar_tensor_tensor(
                out=res[:, hs], in0=pm2[i][:, :], scalar=T.ss2[:, 2:3], in1=P.G[:, w, hs], op0=ALU.mult, op1=ALU.mult))
        S.op("pool", [res, xt], [res], lambda e: e.tensor_tensor(out=res[:, :], in0=res[:, :], in1=xt[:, :], op=ALU.add))
        S.dma("pool", dst_ap, res[:, :], [res], [dstbuf], res)

    def phase_C1(l, src_lat, src_ctx, dst_lat, dst_ctx, do_ctx):
        st = ExitStack()
        T = norm_tiles(st)
        T.ss2 = sb(st, "ss2", [128, 4])
        P.G = sb(st, "Grow", [128, 2, D])
        S.dma("sp", P.G[:, :, :], X.Gd.t.rearrange("p (a b d) -> p a b d", a=2, b=2)[:, :, 0, :], [X.Gd], [P.G], P.G)
        Wg = load_w_bf16(st, "wg", I.w_in[l][:, 3488:5536], 8, 2048)
        Wr = load_w_bf16(st, "wr", I.rw_out[l], 4, D)
        Wl = load_w_bf16(st, "wl", I.gla_out[l], 4, D)
        Wm = load_w_bf16(st, "wm", I.merge_out[l], 8, D)
        hT = sb(st, "c1hT", [128, 8, 128], BF16)
        sg = sb(st, "c1sg", [128, 2048])
        yin = [sb(st, "c1y%d" % i, [128, 512]) for i in range(2)]
        yb = sb(st, "c1yb", [128, 512], BF16)
        yT = [sb(st, "c1yT%d" % i, [128, 4, 128], BF16) for i in range(2)]
        Xs = sb(st, "c1X", [128, D])
        Xb = sb(st, "c1Xb", [128, D], BF16)
        XT = sb(st, "c1XT", [128, 8, 128], BF16)
        res = sb(st, "c1res", [128, D])
        pb = [psb(st, "c1p%d" % i, [128, 512]) for i in range(6)]
        ntile = (2 if do_ctx else 0) + 32
        for ti in range(min(ntile, 2 * ntlC)):
            if do_ctx and ti < 2:
                w, t0, sbuf_, dbuf_ = 1, ti * 128, src_ctx, dst_ctx
                yr0 = t0
            else:
                tt = ti - (2 if do_ctx else 0)
                w, t0, sbuf_, dbuf_ = 0, tt * 128, src_lat, dst_lat
                yr0 = TC + t0
            xt = make_hT(T, [(sbuf_.t[t0:t0 + 128, :], 0, 128)], [sbuf_], w, 0, hT, 0)
            for n in range(4):
                pg = pb[n]

                def f(e, pg=pg, n=n):
                    for dc in range(8):
                        ins = e.matmul(pg[:, :], hT[:, dc, :], Wg[:, dc, n * 512:(n + 1) * 512], start=(dc == 0), stop=(dc == 7))
                    return ins
                S.op("pe", [hT, Wg], [pg], f)
                S.op("act", [pg], [sg], lambda e, pg=pg, n=n: e.activation(out=sg[:, n * 512:(n + 1) * 512], in_=pg[:, :], func=AF.Sigmoid))
            for bi, (ybuf, Wo) in enumerate(((X.yrw, Wr), (X.ygl, Wl))):
                yi = yin[bi]
                S.dma("sp", yi[:, :], ybuf.t[yr0:yr0 + 128, :], [ybuf], [yi], yi)
                S.op("dve", [yi], [yb], lambda e, yi=yi: e.tensor_copy(out=yb[:, :], in_=yi[:, :]))

                def f(e):
                    for c4 in range(4):
                        ins = e.transpose(T.pt[:, c4, :], yb[:, c4 * 128:(c4 + 1) * 128], P.ident_b[:, :])
                    return ins
                S.op("pe", [yb], [T.pt], f)
                S.op("act", [T.pt], [yT[bi]], lambda e, bi=bi: e.activation(out=yT[bi][:, :, :], in_=T.pt[:, 0:4, :], func=AF.Copy))
                for n in range(2):
                    pp_ = pb[4 + n]

                    def f(e, pp_=pp_, n=n, bi=bi, Wo=Wo):
                        for c4 in range(4):
                            ins = e.matmul(pp_[:, :], yT[bi][:, c4, :], Wo[:, c4, n * 512:(n + 1) * 512], start=(c4 == 0), stop=(c4 == 3))
                        return ins
                    S.op("pe", [yT[bi], Wo], [pp_], f)
                    hs = slice(n * 512, (n + 1) * 512)
                    gs = slice(bi * 1024 + n * 512, bi * 1024 + (n + 1) * 512)
                    if bi == 0:
                        S.op("dve", [pp_, sg], [Xs], lambda e, pp_=pp_, hs=hs, gs=gs: e.tensor_tensor(out=Xs[:, hs], in0=pp_[:, :], in1=sg[:, gs], op=ALU.mult))
                    else:
                        S.op("dve", [pp_, sg], [sg], lambda e, pp_=pp_, gs=gs: e.tensor_tensor(out=sg[:, gs], in0=pp_[:, :], in1=sg[:, gs], op=ALU.mult))
                        S.op("pool", [Xs, sg], [Xb], lambda e, hs=hs, gs=gs: e.tensor_tensor(out=Xb[:, hs], in0=Xs[:, hs], in1=sg[:, gs], op=ALU.add))

            def f(e):
                for dc in range(8):
                    ins = e.transpose(T.pt[:, dc, :], Xb[:, dc * 128:(dc + 1) * 128], P.ident_b[:, :])
                return ins
            S.op("pe", [Xb], [T.pt], f)
            S.op("act", [T.pt], [XT], lambda e: e.activation(out=XT[:, :, :], in_=T.pt[:, :, :], func=AF.Copy))
            pm2 = [pb[0], pb[1]]
            for n in range(2):
                def f(e, n=n):
                    for dc in range(8):
                        ins = e.matmul(pm2[n][:, :], XT[:, dc, :], Wm[:, dc, n * 512:(n + 1) * 512], start=(dc == 0), stop=(dc == 7))
                    return ins
                S.op("pe", [XT, Wm], [pm2[n]], f)
            rms_residual(T, pm2, xt, w, 0, dbuf_.t[t0:t0 + 128, :], dbuf_, res)
        S.barrier()
        st.close()

    def phase_C2(l, src_lat, src_ctx, dst_lat, dst_ctx, do_ctx):
        st = ExitStack()
        T = norm_tiles(st, need_xt=False)
        T.ss2 = sb(st, "ss2b", [128, 4])
        P.G = sb(st, "Grow", [128, 2, D])
        S.dma("sp", P.G[:, :, :], X.Gd.t.rearrange("p (a b d) -> p a b d", a=2, b=2)[:, :, 1, :], [X.Gd], [P.G], P.G)
        W1 = load_w_bf16(st, "w1", I.mlp_w1[l], 8, 4 * D)
        W2 = load_w_bf16(st, "w2", I.mlp_w2[l], 32, D)
        hT = [sb(st, "c2hT%d" % i, [128, 8, 256], BF16) for i in range(2)]
        h1 = sb(st, "c2h1", [128, 32, 256], BF16)
        rl = [sb(st, "c2rl%d" % i, [128, 256]) for i in range(2)]
        xk = [[sb(st, "c2xk%d_%d" % (i, j), [128, D]) for j in range(2)] for i in range(2)]
        res = [sb(st, "c2res%d" % i, [128, D]) for i in range(2)]
        pb = [psb(st, "c2p%d" % i, [128, 512]) for i in range(7)]
        tiles = (([(1, 0)] if do_ctx else []) + [(0, i * 256) for i in range(16)])[:ntlC]

        def prep(i):
            w, t0 = tiles[i]
            sbuf_ = src_ctx if w == 1 else src_lat
            for sub in range(2):
                make_hT(T, [(sbuf_.t[t0 + sub * 128:t0 + sub * 128 + 128, :], 0, 128)], [sbuf_], w, 1, hT[i % 2], sub * 128, keep_x=xk[i % 2][sub])
        prep(0)
        k = 0
        for i, (w, t0) in enumerate(tiles):
            dbuf_ = dst_ctx if w == 1 else dst_lat
            h = hT[i % 2]
            for hc in range(32):
                p_ = pb[4 + (k % 3)]
                rr = rl[k % 2]
                k += 1

                def f(e, p_=p_, hc=hc, h=h):
                    for dc in range(8):
                        ins = e.matmul(p_[:, 0:256], W1[:, dc, hc * 128:(hc + 1) * 128], h[:, dc, :], start=(dc == 0), stop=(dc == 7))
                    return ins
                S.op("pe", [W1, h], [p_], f)
                S.op("act", [p_], [rr], lambda e, p_=p_, rr=rr: e.activation(out=rr[:, :], in_=p_[:, 0:256], func=AF.Relu))
                eng = "dve" if hc % 2 == 0 else "pool"
                S.op(eng, [rr], [h1], lambda e, rr=rr, hc=hc: e.tensor_tensor(out=h1[:, hc, :], in0=rr[:, :], in1=rr[:, :], op=ALU.mult))
                if hc == 20 and i + 1 < len(tiles):
                    prep(i + 1)
            for sub in range(2):
                pm2 = [pb[2 * sub], pb[2 * sub + 1]]
                for n in range(2):
                    def f(e, n=n, sub=sub, pm2=pm2):
                        for hc in range(32):
                            ins = e.matmul(pm2[n][:, :], h1[:, hc, sub * 128:(sub + 1) * 128], W2[:, hc, n * 512:(n + 1) * 512], start=(hc == 0), stop=(hc == 31))
                        return ins
                    S.op("pe", [h1, W2], [pm2[n]], f)
                r0 = t0 + sub * 128
                rms_residual(T, pm2, xk[i % 2][sub], w, 1, dbuf_.t[r0:r0 + 128, :], dbuf_, res[sub])
        S.barrier()
        st.close()

    def run():
        src = (I.x_lat, I.x_ctx)
        for l in range(nlayers):
            last = l == 1
            phase_mod(l)
            if stop == "mod%d" % l:
                return
            phase_A(l, src[0], src[1], "rw")
            phase_A(l, src[0], src[1], "gla")
            if stop == "A%d" % l:
                return
            phase_B1(l, 0)
            phase_B1(l, 1)
            if stop == "B1%d" % l:
                return
            phase_B2(l, 0)
            phase_B2(l, 1)
            if stop == "B2%d" % l:
                return
            phase_C1(l, src[0], src[1], X.xm_lat, X.xm_ctx, not last)
            if stop == "C1%d" % l:
                return
            dst = (out_t, None) if last else (X.x1_lat, X.x1_ctx)
            phase_C2(l, X.xm_lat, X.xm_ctx, dst[0], dst[1], not last)
            src = dst
    run()
    S.barrier()
    if dbg:
        pass
    es.close()
    return nc


def prep_inputs(inp, b):
    f = np.ascontiguousarray

    def fm(v, nch):
        sh = v.shape[:-1]
        return f(np.moveaxis(v.reshape(sh + (nch, 128)), -1, -2))
    m = {}
    m["x_lat"] = f(inp["x"][b])
    m["x_ctx"] = f(inp["ctx"][b])
    cf = np.stack([inp["c"][b], inp["c_ctx"]], axis=-1)
    m["cfm"] = f(cf.reshape(8, 128, 2).transpose(1, 0, 2))
    m["ada_w"] = inp["ada_w"]
    m["adab_fm"] = fm(inp["ada_b"], 48)
    m["ada_b"] = inp["ada_b"]
    m["w_in"] = inp["w_in"]
    npre = np.stack([inp["norm_mix_pre"], inp["norm_ffn_pre"]], axis=-1)
    m["npre_fm"] = f(npre.reshape(2, 8, 128, 2).transpose(0, 2, 1, 3))
    m["npost"] = f(np.stack([inp["norm_mix_post"], inp["norm_ffn_post"]], axis=1))
    m["rw_mu_fm"] = fm(inp["rw_mu"], 15)
    m["rw_w0_fm"] = fm(inp["rw_w0"], 4)
    m["rw_a0_fm"] = fm(inp["rw_a0"], 4)
    m["rw_w_up"] = f(inp["rw_w_up"].reshape(2, 128, 512))
    m["rw_a_up"] = f(inp["rw_a_up"].reshape(2, 128, 512))
    m["rw_g_up"] = inp["rw_g_up"]
    kk = np.stack([inp["rw_k_k"], inp["rw_k_a"], inp["rw_r_k"].reshape(2, 512)], axis=-1)
    m["kk_fm"] = f(kk.reshape(2, 4, 128, 3).transpose(0, 2, 1, 3))
    m["rw_gn"] = f(np.stack([inp["rw_gn_w"], inp["rw_gn_b"]], axis=1))
    m["vres_down"] = f(inp["rw_vres_down"][0].reshape(4, 128, 32).transpose(1, 0, 2))
    m["vres_up"] = f(inp["rw_vres_up"][0])
    m["vres_bias"] = f(inp["rw_vres_bias"])
    for k in ("rw_out", "gla_out", "merge_out", "mlp_w1", "mlp_w2", "gla_alpha_up", "gla_norm_w"):
        m[k] = inp[k]
    gc = inp["gla_conv"]
    m["gla_conv_fm"] = f(gc.reshape(2, 3, 8, 128).transpose(0, 3, 2, 1))
    m["gla_abias_fm"] = fm(inp["gla_alpha_bias"], 2)
    return {k: np.ascontiguousarray(v, dtype=np.float32) for k, v in m.items()}


def kernel(**inputs):
    inp = {k: np.asarray(v) for k, v in inputs.items()}
    nc = build()
    in_maps = [prep_inputs(inp, c % 4) for c in range(8)]
    res = run_bass_kernel_spmd(nc, in_maps, core_ids=list(range(8)))
    out = np.stack([np.asarray(res.results[b]["out"]) for b in range(4)], axis=0)
    return out.astype(np.float32)
```
